# Optimizing a Trainium2 kernel written in Bass

```python
import math
import jax, jax.numpy as jnp
from jax import lax
import numpy as np

D_MODEL = 1024
BATCH = 2
SEQ = 8192
DEPTH = 1

MEM_LEN = 256
RET_HEADS = 4
RET_HEAD_DIM = 128
RET_WIDTH = RET_HEADS * RET_HEAD_DIM
DIFF_HEADS = 4
DIFF_HALF_DIM = 64
DIFF_V_DIM = 2 * DIFF_HALF_DIM
DIFF_WIDTH = DIFF_HEADS * DIFF_V_DIM
MIX_WIDTH = RET_WIDTH + DIFF_WIDTH
IN_COLS = 4 * RET_WIDTH + 3 * DIFF_WIDTH
MEM_HEADS = 4
MEM_HEAD_DIM = D_MODEL // MEM_HEADS
D_FF = 2816
CHUNK = 128
Q_BLOCK = 128
EPS = 1e-6
NEG_BIG = -1e30

kernel_name = "hybrid_retention_diffattn_macaron"


def rms_norm(x, g):
    xf = x.astype(jnp.float32)
    y = xf * lax.rsqrt(jnp.mean(xf * xf, axis=-1, keepdims=True) + EPS)
    return (y * g.astype(jnp.float32)).astype(x.dtype)


def group_norm(x, g):
    xf = x.astype(jnp.float32)
    mu = jnp.mean(xf, axis=-1, keepdims=True)
    xc = xf - mu
    y = xc * lax.rsqrt(jnp.mean(xc * xc, axis=-1, keepdims=True) + EPS)
    return (y * g.astype(jnp.float32)).astype(x.dtype)


def swiglu(h, w1, w3, w2):
    return (jax.nn.silu(h @ w1) * (h @ w3)) @ w2


def retention(q, k, v):
    b, s, h, d = q.shape
    nc = s // CHUNK
    log_gamma = jnp.log1p(-jnp.exp2(-5.0 - jnp.arange(h, dtype=jnp.float32)))
    idx = jnp.arange(CHUNK, dtype=jnp.float32)
    rel = idx[:, None] - idx[None, :]
    intra_decay = jnp.where(rel[None] >= 0,
                            jnp.exp(log_gamma[:, None, None] * jnp.maximum(rel, 0.0)[None]), 0.0)
    q = q.astype(jnp.float32).reshape(b, nc, CHUNK, h, d)
    k = k.astype(jnp.float32).reshape(b, nc, CHUNK, h, d) * (d ** -0.5)
    v = v.astype(jnp.float32).reshape(b, nc, CHUNK, h, d)
    scores = jnp.einsum('bnihd,bnjhd->bnhij', q, k) * intra_decay
    o_intra = jnp.einsum('bnhij,bnjhe->bnihe', scores, v)
    k_decay = jnp.exp(log_gamma[None, :] * (CHUNK - 1.0 - idx)[:, None])
    kv = jnp.einsum('bnjhd,jh,bnjhe->bnhde', k, k_decay, v)
    chunk_decay = jnp.exp(log_gamma * CHUNK)[None, :, None, None]

    def step(state, kv_c):
        return chunk_decay * state + kv_c, state

    init = jnp.zeros((b, h, d, d), jnp.float32)
    _, states = lax.scan(step, init, jnp.moveaxis(kv, 1, 0))
    states = jnp.moveaxis(states, 0, 1)
    q_decay = jnp.exp(log_gamma[None, :] * (idx + 1.0)[:, None])
    o_cross = jnp.einsum('bnihd,ih,bnhde->bnihe', q, q_decay, states)
    return (o_intra + o_cross).reshape(b, s, h, d)


def diff_attention(q, k, v, lam):
    b, s, h, _, dh = q.shape
    dv = v.shape[-1]
    nb = s // Q_BLOCK
    slopes = jnp.exp2(-8.0 * jnp.arange(1, h + 1, dtype=jnp.float32) / h)
    scale = dh ** -0.5
    kpos = jnp.arange(s)
    qb = jnp.moveaxis(q.reshape(b, nb, Q_BLOCK, h, 2, dh), 1, 0)

    def block(args):
        q_blk, blk = args
        qpos = blk * Q_BLOCK + jnp.arange(Q_BLOCK)
        dist = (qpos[:, None] - kpos[None, :]).astype(jnp.float32)
        sc = jnp.einsum('bqhmd,bkhmd->bhmqk', q_blk, k).astype(jnp.float32) * scale
        sc = sc - slopes[None, :, None, None, None] * dist
        sc = jnp.where(dist >= 0, sc, NEG_BIG)
        p = jax.nn.softmax(sc, axis=-1)
        a = p[:, :, 0] - lam * p[:, :, 1]
        return jnp.einsum('bhqk,bkhe->bqhe', a.astype(v.dtype), v)

    out = lax.map(block, (qb, jnp.arange(nb)))
    return jnp.moveaxis(out, 0, 1).reshape(b, s, h, dv)


def hybrid_mixer(h, w_in, ret_norm, diff_q_norm, diff_k_norm,
                 lambda_q1, lambda_k1, lambda_q2, lambda_k2, diff_norm, w_out, lam_init):
    b, s, _ = h.shape
    proj = h @ w_in
    cuts = [RET_WIDTH, 2 * RET_WIDTH, 3 * RET_WIDTH, 4 * RET_WIDTH,
            4 * RET_WIDTH + DIFF_WIDTH, 4 * RET_WIDTH + 2 * DIFF_WIDTH]
    rq, rk, rv, rg, dq, dk, dvv = jnp.split(proj, cuts, axis=-1)
    shp_r = (b, s, RET_HEADS, RET_HEAD_DIM)
    ro = retention(rq.reshape(shp_r), rk.reshape(shp_r), rv.reshape(shp_r))
    ro = group_norm(ro, ret_norm).reshape(b, s, RET_WIDTH).astype(h.dtype)
    ro = jax.nn.silu(rg) * ro
    shp_d = (b, s, DIFF_HEADS, 2, DIFF_HALF_DIM)
    dq = rms_norm(dq.reshape(shp_d), diff_q_norm)
    dk = rms_norm(dk.reshape(shp_d), diff_k_norm)
    dvv = dvv.reshape(b, s, DIFF_HEADS, DIFF_V_DIM)
    lam = (jnp.exp(jnp.sum(lambda_q1.astype(jnp.float32) * lambda_k1.astype(jnp.float32)))
           - jnp.exp(jnp.sum(lambda_q2.astype(jnp.float32) * lambda_k2.astype(jnp.float32)))
           + lam_init)
    do = diff_attention(dq, dk, dvv, lam)
    do = (rms_norm(do, diff_norm) * (1.0 - lam_init)).reshape(b, s, DIFF_WIDTH).astype(h.dtype)
    return jnp.concatenate([ro, do], axis=-1) @ w_out


def memory_attention(h, mem_n, w_q, w_kv, w_o, q_norm, k_norm):
    b, s, _ = h.shape
    m = mem_n.shape[1]
    q = rms_norm((h @ w_q).reshape(b, s, MEM_HEADS, MEM_HEAD_DIM), q_norm)
    kv = (mem_n @ w_kv).reshape(b, m, 2, MEM_HEADS, MEM_HEAD_DIM)
    k = rms_norm(kv[:, :, 0], k_norm)
    v = kv[:, :, 1]
    sc = jnp.einsum('bqhd,bkhd->bhqk', q, k).astype(jnp.float32) * (MEM_HEAD_DIM ** -0.5)
    p = jax.nn.softmax(sc, axis=-1).astype(v.dtype)
    o = jnp.einsum('bhqk,bkhd->bqhd', p, v).reshape(b, s, D_MODEL)
    return o @ w_o


def setup_inputs(seed: int = 0) -> dict:
    key = jax.random.key(seed)
    ks = iter(jax.random.split(key, 40))

    def dense(fan_in, fan_out):
        return jax.random.normal(next(ks), (DEPTH, fan_in, fan_out), jnp.float32) * fan_in ** -0.5

    def gain(n):
        return 1.0 + 0.02 * jax.random.normal(next(ks), (DEPTH, n), jnp.float32)

    def small(n):
        return 0.1 * jax.random.normal(next(ks), (DEPTH, n), jnp.float32)

    return {
        "x": jax.random.normal(next(ks), (BATCH, SEQ, D_MODEL), jnp.float32),
        "mem": jax.random.normal(next(ks), (BATCH, MEM_LEN, D_MODEL), jnp.float32),
        "norm_ffn1": gain(D_MODEL),
        "ffn1_w1": dense(D_MODEL, D_FF),
        "ffn1_w3": dense(D_MODEL, D_FF),
        "ffn1_w2": dense(D_FF, D_MODEL),
        "norm_mix": gain(D_MODEL),
        "w_in": dense(D_MODEL, IN_COLS),
        "ret_norm": gain(RET_HEAD_DIM),
        "diff_q_norm": gain(DIFF_HALF_DIM),
        "diff_k_norm": gain(DIFF_HALF_DIM),
        "lambda_q1": small(DIFF_HALF_DIM),
        "lambda_k1": small(DIFF_HALF_DIM),
        "lambda_q2": small(DIFF_HALF_DIM),
        "lambda_k2": small(DIFF_HALF_DIM),
        "diff_norm": gain(DIFF_V_DIM),
        "w_out": dense(MIX_WIDTH, D_MODEL),
        "norm_mem_q": gain(D_MODEL),
        "norm_mem_kv": gain(D_MODEL),
        "mem_w_q": dense(D_MODEL, D_MODEL),
        "mem_w_kv": dense(D_MODEL, 2 * D_MODEL),
        "mem_q_norm": gain(MEM_HEAD_DIM),
        "mem_k_norm": gain(MEM_HEAD_DIM),
        "mem_w_o": dense(D_MODEL, D_MODEL),
        "norm_ffn2": gain(D_MODEL),
        "ffn2_w1": dense(D_MODEL, D_FF),
        "ffn2_w3": dense(D_MODEL, D_FF),
        "ffn2_w2": dense(D_FF, D_MODEL),
        "norm_out": gain(D_MODEL),
    }


def reference(x, mem, norm_ffn1, ffn1_w1, ffn1_w3, ffn1_w2, norm_mix, w_in, ret_norm,
              diff_q_norm, diff_k_norm, lambda_q1, lambda_k1, lambda_q2, lambda_k2, diff_norm,
              w_out, norm_mem_q, norm_mem_kv, mem_w_q, mem_w_kv, mem_q_norm, mem_k_norm, mem_w_o,
              norm_ffn2, ffn2_w1, ffn2_w3, ffn2_w2, norm_out):
    for l in range(DEPTH):
        lam_init = 0.8 - 0.6 * math.exp(-0.3 * l)
        x = x + 0.5 * swiglu(rms_norm(x, norm_ffn1[l]), ffn1_w1[l], ffn1_w3[l], ffn1_w2[l])
        x = x + hybrid_mixer(rms_norm(x, norm_mix[l]), w_in[l], ret_norm[l], diff_q_norm[l],
                             diff_k_norm[l], lambda_q1[l], lambda_k1[l], lambda_q2[l],
                             lambda_k2[l], diff_norm[l], w_out[l], lam_init)
        x = x + memory_attention(rms_norm(x, norm_mem_q[l]), rms_norm(mem, norm_mem_kv[l]),
                                 mem_w_q[l], mem_w_kv[l], mem_w_o[l], mem_q_norm[l], mem_k_norm[l])
        x = x + 0.5 * swiglu(rms_norm(x, norm_ffn2[l]), ffn2_w1[l], ffn2_w3[l], ffn2_w2[l])
        x = rms_norm(x, norm_out[l])
    return x
```

```python
import math
import contextlib
import numpy as np
import ml_dtypes
import concourse.bass as bass
import concourse.mybir as mybir
from concourse.bass_utils import run_bass_kernel_spmd

F32 = mybir.dt.float32
BF16 = mybir.dt.bfloat16
AF = mybir.ActivationFunctionType
ALU = mybir.AluOpType
AX = mybir.AxisListType

D = 1024
DFF = 2816
NT = 16
EPS = 1e-6
LAM_INIT = 0.8 - 0.6 * math.exp(-0.3 * 0)
NDMA = 8
import os
MIXOPT = set(os.environ.get("MIXOPT", "").split(","))
SK = set(os.environ.get("SKIP", "").split(","))
GROUPS = [[0, 1, 2, 3], [4, 5, 6, 7]]


class Sched:
    COMPUTE = ("pe", "act", "dve", "pool")

    def __init__(self, nc):
        self.nc = nc
        self.ops = []
        self.lastw = {}
        self.readers = {}
        self.pending = {}
        self.bar_from = 0

    def barrier(self):
        tails = set()
        lastc = {}
        for i in range(len(self.ops)):
            o = self.ops[i]
            if o["kind"] == "c":
                lastc[o["q"]] = i
            elif i >= self.bar_from:
                tails.add(i)
        tails.update(lastc.values())
        self.bar_from = len(self.ops)
        for q in ("sp", "pe", "act", "dve", "pool"):
            self.pending[q] = set(tails) | self.pending.get(q, set())

    def _add(self, q, kind, fn, r, w):
        idx = len(self.ops)
        deps = set()
        if q in self.pending:
            deps |= self.pending.pop(q)
        for k in r:
            lw = self.lastw.get(k)
            if lw is not None:
                deps.add(lw)
        for k in w:
            lw = self.lastw.get(k)
            if lw is not None:
                deps.add(lw)
            rd = self.readers.get(k)
            if rd:
                for v in rd.values():
                    if isinstance(v, list):
                        deps.update(v)
                    else:
                        deps.add(v)
        self.ops.append(dict(q=q, kind=kind, fn=fn, deps=deps))
        for k in r:
            rd = self.readers.setdefault(k, {})
            if kind == "c":
                rd[q] = idx
            else:
                rd.setdefault("x", []).append(idx)
        for k in w:
            self.lastw[k] = idx
            self.readers[k] = {}
        return idx

    def op(self, q, fn, r=(), w=()):
        return self._add(q, "c", fn, r, w)

    def dma(self, q, out, in_, r=(), w=()):
        return self._add(q, "dma", lambda e: e.dma_start(out=out, in_=in_), r, w)

    def cc(self, kind, alu, ins, outs, r=(), w=()):
        if "nocc" in MIXOPT:
            return None
        return self._add("pool", "cc", lambda e: e.collective_compute(
            kind, alu, replica_groups=GROUPS, ins=ins, outs=outs), r, w)

    def emit(self, final_wait_ops):
        nc = self.nc
        ops = self.ops
        needed = set()
        for o in ops:
            for d in o["deps"]:
                p = ops[d]
                if p["kind"] == "c" and p["q"] == "pe" and o["q"] == "pe" and o["kind"] == "c":
                    continue
                needed.add(d)
        needed.update(final_wait_ops)
        cnt = {q: 0 for q in self.COMPUTE}
        dcnt = {}
        drr = {}
        ncc = 0
        for i, o in enumerate(ops):
            q = o["q"]
            if o["kind"] == "c":
                if i in needed:
                    cnt[q] += 1
                    o["sem"] = ("c", q)
                    o["val"] = cnt[q]
                else:
                    o["sem"] = None
            elif o["kind"] == "dma":
                k = drr.get(q, 0) % NDMA
                drr[q] = drr.get(q, 0) + 1
                prev = dcnt.get((q, k), 0)
                o["prev"] = prev
                dcnt[(q, k)] = prev + 16
                o["sem"] = ("d", q, k)
                o["val"] = prev + 16
            else:
                o["sem"] = ("cc", ncc)
                o["val"] = 1
                ncc += 1
        semnames = set(o["sem"] for o in ops if o["sem"] is not None)
        stack = contextlib.ExitStack()
        semh = {}
        for sn in sorted(semnames, key=str):
            semh[sn] = stack.enter_context(nc.semaphore("s_" + "_".join(str(x) for x in sn)))
        queues = ("sp", "pe", "act", "dve", "pool")
        per_q = {q: [i for i, o in enumerate(ops) if o["q"] == q] for q in queues}

        def run(eng, q):
            seen = {}
            for i in per_q[q]:
                o = ops[i]
                waits = {}
                for d in o["deps"]:
                    p = ops[d]
                    if p["kind"] == "c" and p["q"] == "pe" and q == "pe" and o["kind"] == "c":
                        continue
                    s = p["sem"]
                    waits[s] = max(waits.get(s, 0), p["val"])
                if o["kind"] == "dma" and o["prev"] > 0:
                    s = o["sem"]
                    waits[s] = max(waits.get(s, 0), o["prev"])
                for s, v in waits.items():
                    if seen.get(s, 0) < v:
                        eng.wait_ge(semh[s], v)
                        seen[s] = v
                ins = o["fn"](eng)
                if o["sem"] is not None:
                    ins.then_inc(semh[o["sem"]], 16 if o["kind"] == "dma" else 1)
            if q == "sp":
                for i in final_wait_ops:
                    o = ops[i]
                    if seen.get(o["sem"], 0) < o["val"]:
                        eng.wait_ge(semh[o["sem"]], o["val"])
                        seen[o["sem"]] = o["val"]

        with stack, nc.Block() as block:
            @block.sync
            def _(e):
                run(e, "sp")

            @block.tensor
            def _(e):
                run(e, "pe")

            @block.scalar
            def _(e):
                run(e, "act")

            @block.vector
            def _(e):
                run(e, "dve")

            @block.gpsimd
            def _(e):
                run(e, "pool")


def build_program(stages=("kv", "ffn1", "mix", "mem", "ffn2"), dbg=()):
    nc = bass.Bass("TRN2", target_bir_lowering=False)
    S = Sched(nc)

    def din(name, shape, dt=F32):
        return nc.dram_tensor(name, shape, dt, kind="ExternalInput").ap()

    x_d = din("x", [2048, D])
    mem_d = din("mem", [256, D])
    gains_d = din("gains", [6, D])
    wf = {}
    for f in ("a", "b"):
        wf[f] = (din("w1" + f, [D, DFF]), din("w3" + f, [D, DFF]), din("w2" + f, [DFF, D]))
    wh_d = din("w_head", [D, 896])
    woh_d = din("wout_head", [256, D])
    wq_d = din("wq", [D, D])
    wkv_d = din("wkv", [D, 2 * D])
    wo_d = din("wo", [D, D])
    retn_d = din("ret_norm", [128, 1])
    dfn_d = din("diff_norm", [128, 1])
    qkg_d = din("qk_gain", [1, 256])
    lamv_d = din("lam_vecs", [1, 256])
    mqn_d = din("mem_q_norm", [1, 256])
    mkn_d = din("mem_k_norm", [1, 256])
    decT_d = din("decT", [128, 128])
    qdec_d = din("qdec", [128, 512])
    kdec_d = din("kdec", [128, 1])
    g128_d = din("g128", [128, 1])
    kaug_d = din("kaug_tm", [128, 64])
    qaug_d = din("qaug_tm", [128, 256])
    biasT_d = din("biasT", [128, 64])
    maskT_d = din("maskT", [128, 2048])
    ident_d = din("ident", [128, 128])
    y_d = nc.dram_tensor("y", [2048, D], F32, kind="ExternalOutput").ap()
    dbg_d = {}
    for name, shape in dbg:
        dbg_d[name] = nc.dram_tensor(name, list(shape), F32, kind="ExternalOutput").ap()

    cin = [nc.dram_tensor(f"cin{s}", [1024, 512], BF16, kind="Internal").ap() for s in range(4)]
    cg = [nc.dram_tensor(f"cg{s}", [4096, 512], BF16, kind="Internal").ap() for s in range(4)]
    Yd = [nc.dram_tensor(f"Yd{s}", [2048, D], F32, kind="Internal").ap() for s in range(4)]
    Zd = [nc.dram_tensor(f"Zd{s}", [512, D], F32, kind="Internal").ap() for s in range(4)]
    xsp_d = nc.dram_tensor("xspill", [128, NT * D], F32, kind="Internal").ap()

    st = contextlib.ExitStack()

    def sb(name, shape, dt):
        return st.enter_context(nc.sbuf_tensor(name, shape, dt))

    ARENA = 73728
    arena = sb("arena", [128, ARENA], BF16)

    def abf(off, n):
        return arena[:, off:off + n]

    def af32(off, n):
        return arena[:, off:off + 2 * n].bitcast(F32)

    xs = af32(0, NT * D).rearrange("p (t d) -> p t d", t=NT)
    HT, GT, W13, W2P = 32768, 49152, 57344, 65536
    hT = abf(HT, 16384).rearrange("p (k t) -> p k t", k=8)
    gT = abf(GT, 8192).rearrange("p (c t) -> p c t", c=4)
    w13b = [abf(W13 + i * 2048, 2048).rearrange("p (k c) -> p k c", k=8) for i in range(4)]
    w2b = [abf(W2P + i * 4096, 4096).rearrange("p (c d) -> p c d", c=4) for i in range(2)]

    gbc = sb("gbc", [128, D], F32)
    hn = sb("hn", [128, 2, D], BF16)
    ident = sb("ident_sb", [128, 128], BF16)
    ones_bf = sb("ones_bf", [128, 128], BF16)
    ones_f = sb("ones_f", [128, 128], F32)
    ones512 = sb("ones512", [128, 512], F32)
    stats = sb("stats", [128, 512], F32)
    sil = sb("sil", [128, 2, 512], F32)
    junk = sb("junk", [128, D], BF16)
    mkT = sb("mkT", [128, 8, 256], BF16)
    mv = sb("mv", [128, 2, D], BF16)
    cst = sb("cst", [128, 1024], F32)
    decT = sb("decT_sb", [128, 128], F32)
    qdec = sb("qdec_sb", [128, 512], F32)
    biasT = sb("biasT_sb", [128, 64], F32)
    state_f = sb("state_f", [128, 128], F32)
    ps = [st.enter_context(nc.psum_tensor(f"ps{i}", [128, 512], F32)) for i in range(8)]

    def psbf(i):
        return ps[i][:, :].bitcast(BF16)

    C_RETN, C_DFN, C_KDEC, C_G128, C_NLAM, C_TMP = 0, 1, 2, 3, 4, 8
    C_QKG = 16
    C_LAMV = 272
    C_MQN = 528
    C_MKN = 784
    mkn_t = sb("mkn_t", [128, 256], F32)

    mst = sb("mst", [128, 256], F32)
    rot_ctr = [0]

    def rot_cols(n):
        o = rot_ctr[0]
        if o + n > 256:
            o = 0
        rot_ctr[0] = o + n
        return o

    stat_ctr = [0]

    def stat_cols(n):
        o = stat_ctr[0]
        stat_ctr[0] += n
        assert stat_ctr[0] <= 512
        return o

    S.dma("pool", ident[:, :], ident_d, w=["ident"])
    S.op("dve", lambda e: e.memset(ones_bf[:, :], 1.0), w=["ones_bf"])
    S.op("dve", lambda e: e.memset(ones_f[:, :], 1.0), w=["ones_f"])
    S.op("dve", lambda e: e.memset(ones512[:, :], 1.0), w=["ones512"])
    S.dma("sp", cst[:, C_RETN:C_RETN + 1], retn_d, w=["cst"])
    S.dma("sp", cst[:, C_DFN:C_DFN + 1], dfn_d, w=["cst"])
    S.dma("sp", cst[:, C_KDEC:C_KDEC + 1], kdec_d, w=["cst"])
    S.dma("sp", cst[:, C_G128:C_G128 + 1], g128_d, w=["cst"])
    S.dma("sp", cst[:, C_QKG:C_QKG + 256], qkg_d.partition_broadcast(128), w=["cst"])
    S.dma("sp", cst[:, C_LAMV:C_LAMV + 256], lamv_d.partition_broadcast(128), w=["cst"])
    S.dma("sp", cst[:, C_MQN:C_MQN + 256], mqn_d.partition_broadcast(128), w=["cst"])
    S.dma("sp", mkn_t[:, :], mkn_d.partition_broadcast(128), w=["mkn"])
    S.dma("sp", decT[:, :], decT_d, w=["decT"])
    S.dma("sp", qdec[:, :], qdec_d, w=["qdec"])
    S.dma("sp", biasT[:, :], biasT_d, w=["biasT"])
    for t in range(NT):
        S.dma("sp", xs[:, t, :], x_d[t * 128:(t + 1) * 128, :], w=[("xs", t)])

    def load_gain(row):
        S.dma("sp", gbc[:, :], gains_d[row:row + 1, :].partition_broadcast(128), w=["gbc"])

    cp_tog = [0]

    def evac_copy(out, in_, r, w):
        cp_tog[0] ^= 1
        if cp_tog[0]:
            S.op("act", lambda e: e.activation(out=out, in_=in_, func=AF.Copy), r=r, w=w)
        else:
            S.op("dve", lambda e: e.tensor_copy(out=out, in_=in_), r=r, w=w)

    def rms_rstd(src_tiles, ncols_total, inv_n, tag):
        n = len(src_tiles)
        c_ss = stat_cols(n)
        c_rs = stat_cols(n)
        for i, (ap, rk) in enumerate(src_tiles):
            S.op("act", lambda e, ap=ap, i=i: e.activation(
                out=junk[:, 0:ap.shape[-1]], in_=ap, func=AF.Square,
                accum_out=stats[:, c_ss + i:c_ss + i + 1]),
                r=rk, w=["junk", ("st", c_ss + i)])
        keys_ss = [("st", c_ss + i) for i in range(n)]
        keys_rs = [("st", c_rs + i) for i in range(n)]
        S.op("dve", lambda e: e.tensor_scalar(
            out=stats[:, c_rs:c_rs + n], in0=stats[:, c_ss:c_ss + n], scalar1=inv_n, scalar2=EPS,
            op0=ALU.mult, op1=ALU.add), r=keys_ss, w=keys_rs)
        S.op("act", lambda e: e.activation(out=stats[:, c_rs:c_rs + n], in_=stats[:, c_rs:c_rs + n],
                                           func=AF.Ln), r=keys_rs, w=keys_rs)
        S.op("act", lambda e: e.activation(out=stats[:, c_rs:c_rs + n], in_=stats[:, c_rs:c_rs + n],
                                           func=AF.Exp, scale=-0.5), r=keys_rs, w=keys_rs)
        return c_rs

    def norm_transpose(src, src_keys, ntiles, dstT, dst_key, pbank):
        c_rs = rms_rstd([(src[:, t, :], [src_keys(t)]) for t in range(ntiles)], ntiles, 1.0 / D, "n")
        for t in range(ntiles):
            hb = t % 2
            S.op("dve", lambda e, t=t, hb=hb: e.scalar_tensor_tensor(
                out=hn[:, hb, :], in0=src[:, t, :], scalar=stats[:, c_rs + t:c_rs + t + 1],
                in1=gbc[:, :], op0=ALU.mult, op1=ALU.mult),
                r=[src_keys(t), ("st", c_rs + t), "gbc"], w=[("hn", hb)])
            pb = pbank[t % len(pbank)]
            pv = psbf(pb)
            for kc in range(8):
                S.op("pe", lambda e, kc=kc, hb=hb, pv=pv: e.transpose(
                    out=pv[:, kc * 128:(kc + 1) * 128], in_=hn[:, hb, kc * 128:(kc + 1) * 128],
                    identity=ident[:, :]), r=[("hn", hb), "ident"], w=[("ps", pb)])
            evac_copy(dstT[:, :, t * 128:(t + 1) * 128],
                      pv[:, 0:1024].rearrange("p (k t) -> p k t", k=8),
                      r=[("ps", pb)], w=[(dst_key, t)])

    if "kv" in stages:
        memx = af32(GT, 2 * D).rearrange("p (t d) -> p t d", t=2)
        for t in range(2):
            S.dma("sp", memx[:, t, :], mem_d[t * 128:(t + 1) * 128, :], w=[("memx", t)])
        load_gain(3)
        memT = hT
        norm_transpose(memx, lambda t: ("memx", t), 2, memT, "hT", [6, 7])
        for cb in range(4):
            wt = abf(W13 + (cb % 2) * 4096, 4096).rearrange("p (k c) -> p k c", k=8)
            S.dma("pool", wt, wkv_d[:, cb * 512:(cb + 1) * 512].rearrange("(k p) c -> p k c", p=128),
                  w=[("w13", (cb % 2) * 2), ("w13", (cb % 2) * 2 + 1)])
            for mt in range(2):
                pb = 4 + mt
                for kc in range(8):
                    S.op("pe", lambda e, kc=kc, mt=mt, wt=wt, pb=pb: e.matmul(
                        out=ps[pb][:, :], lhsT=memT[:, kc, mt * 128:(mt + 1) * 128], rhs=wt[:, kc, :],
                        start=(kc == 0), stop=(kc == 7)),
                        r=[("hT", mt), ("w13", (cb % 2) * 2), ("w13", (cb % 2) * 2 + 1)], w=[("ps", pb)])
                if cb < 2:
                    c_ss = stat_cols(2)
                    c_rs = stat_cols(2)
                    for hh in range(2):
                        S.op("act", lambda e, hh=hh, pb=pb, c_ss=c_ss: e.activation(
                            out=junk[:, 0:256], in_=ps[pb][:, hh * 256:(hh + 1) * 256], func=AF.Square,
                            accum_out=stats[:, c_ss + hh:c_ss + hh + 1]),
                            r=[("ps", pb)], w=["junk", ("st", c_ss + hh)])
                    kss = [("st", c_ss), ("st", c_ss + 1)]
                    krs = [("st", c_rs), ("st", c_rs + 1)]
                    S.op("dve", lambda e, c_ss=c_ss, c_rs=c_rs: e.tensor_scalar(
                        out=stats[:, c_rs:c_rs + 2], in0=stats[:, c_ss:c_ss + 2], scalar1=1.0 / 256,
                        scalar2=EPS, op0=ALU.mult, op1=ALU.add), r=kss, w=krs)
                    S.op("act", lambda e, c_rs=c_rs: e.activation(
                        out=stats[:, c_rs:c_rs + 2], in_=stats[:, c_rs:c_rs + 2], func=AF.Ln), r=krs, w=krs)
                    S.op("act", lambda e, c_rs=c_rs: e.activation(
                        out=stats[:, c_rs:c_rs + 2], in_=stats[:, c_rs:c_rs + 2], func=AF.Exp, scale=-0.5),
                        r=krs, w=krs)
                    for hh in range(2):
                        S.op("dve", lambda e, hh=hh, pb=pb, c_rs=c_rs: e.scalar_tensor_tensor(
                            out=hn[:, 0, hh * 256:(hh + 1) * 256], in0=ps[pb][:, hh * 256:(hh + 1) * 256],
                            scalar=stats[:, c_rs + hh:c_rs + hh + 1], in1=mkn_t[:, :],
                            op0=ALU.mult, op1=ALU.mult),
                            r=[("ps", pb), ("st", c_rs + hh), "mkn"], w=[("hn", 0)])
                    pv = psbf(7 if pb != 7 else 6)
                    pbt = 7 if pb != 7 else 6
                    for j in range(4):
                        S.op("pe", lambda e, j=j, pv=pv: e.transpose(
                            out=pv[:, j * 128:(j + 1) * 128], in_=hn[:, 0, j * 128:(j + 1) * 128],
                            identity=ident[:, :]), r=[("hn", 0), "ident"], w=[("ps", pbt)])
                    evac_copy(mkT[:, cb * 4:(cb + 1) * 4, mt * 128:(mt + 1) * 128],
                              pv[:, 0:512].rearrange("p (k t) -> p k t", k=4),
                              r=[("ps", pbt)], w=["mkT"])
                else:
                    evac_copy(mv[:, mt, (cb - 2) * 512:(cb - 1) * 512], ps[pb][:, :],
                              r=[("ps", pb)], w=["mv"])

        S.barrier()

    PIECES = [(0, 4), (4, 4), (8, 4), (12, 4), (16, 3), (19, 3)]

    def ffn(which, gain_row):
        w1_d, w3_d, w2_d = wf[which]
        load_gain(gain_row)
        norm_transpose(xs, lambda t: ("xs", t), NT, hT, "hT", [6, 7])
        hT_keys = [("hT", t) for t in range(NT)]
        sub_ctr = [0]
        for pi, (c0, ncs) in enumerate(PIECES):
            w2t = w2b[pi % 2]
            S.dma("pool", w2t[:, 0:ncs, :],
                  w2_d[c0 * 128:(c0 + ncs) * 128, :].rearrange("(c p) d -> p c d", p=128),
                  w=[("w2p", pi % 2)])
            subs = [(c0 + i, min(2, ncs - i)) for i in range(0, ncs, 2)]
            for (sc, sn) in subs:
                sbi = sub_ctr[0] % 2
                sub_ctr[0] += 1
                w1t, w3t = w13b[sbi * 2], w13b[sbi * 2 + 1]
                S.dma("pool", w1t[:, :, 0:sn * 128],
                      w1_d[:, sc * 128:(sc + sn) * 128].rearrange("(k p) c -> p k c", p=128),
                      w=[("w13", sbi * 2)])
                S.dma("pool", w3t[:, :, 0:sn * 128],
                      w3_d[:, sc * 128:(sc + sn) * 128].rearrange("(k p) c -> p k c", p=128),
                      w=[("w13", sbi * 2 + 1)])
                for cc in range(sn):
                    cl = sc + cc - c0
                    for tg in range(4):
                        bu, bv = (0, 1) if (tg % 2 == 0) else (2, 3)
                        for kc in range(8):
                            S.op("pe", lambda e, kc=kc, cc=cc, tg=tg, w1t=w1t, bu=bu: e.matmul(
                                out=ps[bu][:, :], lhsT=w1t[:, kc, cc * 128:(cc + 1) * 128],
                                rhs=hT[:, kc, tg * 512:(tg + 1) * 512], start=(kc == 0), stop=(kc == 7)),
                                r=[("w13", sbi * 2)] + hT_keys[tg * 4:tg * 4 + 4], w=[("ps", bu)])
                        for kc in range(8):
                            S.op("pe", lambda e, kc=kc, cc=cc, tg=tg, w3t=w3t, bv=bv: e.matmul(
                                out=ps[bv][:, :], lhsT=w3t[:, kc, cc * 128:(cc + 1) * 128],
                                rhs=hT[:, kc, tg * 512:(tg + 1) * 512], start=(kc == 0), stop=(kc == 7)),
                                r=[("w13", sbi * 2 + 1)] + hT_keys[tg * 4:tg * 4 + 4], w=[("ps", bv)])
                        sb_i = tg % 2
                        S.op("act", lambda e, bu=bu, sb_i=sb_i: e.activation(
                            out=sil[:, sb_i, :], in_=ps[bu][:, :], func=AF.Silu),
                            r=[("ps", bu)], w=[("sil", sb_i)])
                        S.op("dve", lambda e, bv=bv, sb_i=sb_i, cl=cl, tg=tg: e.tensor_tensor(
                            out=gT[:, cl, tg * 512:(tg + 1) * 512], in0=sil[:, sb_i, :], in1=ps[bv][:, :],
                            op=ALU.mult), r=[("sil", sb_i), ("ps", bv)], w=[("gT", cl, tg)])
            for t in range(NT):
                for half in range(2):
                    pb = 4 + (t * 2 + half) % 4
                    for cl in range(ncs):
                        S.op("pe", lambda e, cl=cl, t=t, half=half, pb=pb, w2t=w2t, ncs=ncs: e.matmul(
                            out=ps[pb][:, :], lhsT=gT[:, cl, t * 128:(t + 1) * 128],
                            rhs=w2t[:, cl, half * 512:(half + 1) * 512], start=(cl == 0), stop=(cl == ncs - 1)),
                            r=[("gT", cl, t // 4), ("w2p", pi % 2)], w=[("ps", pb)])
                    S.op("dve", lambda e, t=t, half=half, pb=pb: e.scalar_tensor_tensor(
                        out=xs[:, t, half * 512:(half + 1) * 512], in0=ps[pb][:, :], scalar=0.5,
                        in1=xs[:, t, half * 512:(half + 1) * 512], op0=ALU.mult, op1=ALU.add),
                        r=[("ps", pb), ("xs", t)], w=[("xs", t)])

    if "ffn1" in stages:
        ffn("a", 0)

    if "mix" in stages:
        mixer(nc, S, locals())

    if "mem" in stages:
        mem_attn(nc, S, locals())

    if "ffn2" in stages:
        ffn("b", 4)

    S.barrier()
    load_gain(5)
    c_rs = rms_rstd([(xs[:, t, :], [("xs", t)]) for t in range(NT)], NT, 1.0 / D, "o")
    outs = []
    yb = sil[:, :, :].rearrange("p a b -> p (a b)")
    ystage = [yb, af32(GT, D)]
    for t in range(NT):
        ysb = ystage[t % 2]
        S.op("dve", lambda e, t=t, ysb=ysb: e.scalar_tensor_tensor(
            out=ysb, in0=xs[:, t, :], scalar=stats[:, c_rs + t:c_rs + t + 1], in1=gbc[:, :],
            op0=ALU.mult, op1=ALU.mult), r=[("xs", t), ("st", c_rs + t), "gbc"], w=[("ystage", t % 2)])
        outs.append(S.dma("sp", y_d[t * 128:(t + 1) * 128, :], ysb, r=[("ystage", t % 2)]))
    S.emit(outs)
    st.close()
    return nc


def mixer(nc, S, L):
    abf, af32, ps, psbf = L["abf"], L["af32"], L["ps"], L["psbf"]
    xs, hT, cst, stats = L["xs"], L["hT"], L["cst"], L["stats"]
    ident, ones_bf, ones_f = L["ident"], L["ones_bf"], L["ones_f"]
    decT, qdec, biasT, state_f = L["decT"], L["qdec"], L["biasT"], L["state_f"]
    cin, cg, Yd, Zd, xsp_d = L["cin"], L["cg"], L["Yd"], L["Zd"], L["xsp_d"]
    load_gain, norm_transpose, evac_copy = L["load_gain"], L["norm_transpose"], L["evac_copy"]
    C_RETN, C_DFN, C_KDEC, C_G128, C_NLAM, C_TMP = (L[k] for k in
                                                   ("C_RETN", "C_DFN", "C_KDEC", "C_G128", "C_NLAM", "C_TMP"))
    C_QKG, C_LAMV = L["C_QKG"], L["C_LAMV"]
    HT = L["HT"]
    mst = L["mst"]
    rot = L["rot_cols"]
    C_DFN08 = 5

    load_gain(1)
    norm_transpose(xs, lambda t: ("xs", t), NT, hT, "hT", [6, 7])
    S.dma("sp", xsp_d, xs.rearrange("p t d -> p (t d)"), r=[("xs", t) for t in range(NT)], w=["xspill"])
    for s in range(4):
        S.dma("sp", cin[s].rearrange("(p k) t -> p k t", k=8), hT[:, :, s * 512:(s + 1) * 512],
              r=[("hT", t) for t in range(4 * s, 4 * s + 4)], w=[("cin", s)])
        S.cc("AllGather", ALU.bypass, [cin[s]], [cg[s]], r=[("cin", s)], w=[("cg", s)])
    S.op("dve", lambda e: e.tensor_scalar(out=cst[:, C_QKG:C_QKG + 128], in0=cst[:, C_QKG:C_QKG + 128],
                                          scalar1=0.125, scalar2=None, op0=ALU.mult), r=["cst"], w=["cst"])
    S.op("dve", lambda e: e.tensor_scalar(out=cst[:, C_DFN08:C_DFN08 + 1], in0=cst[:, C_DFN:C_DFN + 1],
                                          scalar1=1.0 - LAM_INIT, scalar2=None, op0=ALU.mult), r=["cst"], w=["cst"])
    for j in range(2):
        S.op("dve", lambda e, j=j: e.tensor_tensor(
            out=L["junk"][:, 0:64], in0=cst[:, C_LAMV + 128 * j:C_LAMV + 128 * j + 64],
            in1=cst[:, C_LAMV + 128 * j + 64:C_LAMV + 128 * j + 128], op=ALU.mult), r=["cst"], w=["junk"])
        S.op("dve", lambda e, j=j: e.tensor_reduce(
            out=cst[:, C_TMP + j:C_TMP + j + 1], in_=L["junk"][:, 0:64], axis=AX.X, op=ALU.add),
            r=["junk"], w=["cst"])
    S.op("act", lambda e: e.activation(out=cst[:, C_TMP:C_TMP + 2], in_=cst[:, C_TMP:C_TMP + 2], func=AF.Exp),
         r=["cst"], w=["cst"])
    S.op("dve", lambda e: e.tensor_tensor(out=cst[:, C_NLAM:C_NLAM + 1], in0=cst[:, C_TMP + 1:C_TMP + 2],
                                          in1=cst[:, C_TMP:C_TMP + 1], op=ALU.subtract), r=["cst"], w=["cst"])
    S.op("dve", lambda e: e.tensor_scalar(out=cst[:, C_NLAM:C_NLAM + 1], in0=cst[:, C_NLAM:C_NLAM + 1],
                                          scalar1=-LAM_INIT, scalar2=None, op0=ALU.add), r=["cst"], w=["cst"])

    S.barrier()
    KTA = [abf(0, 8192), abf(8192, 8192)]
    VC = abf(16384, 8192).rearrange("p (k d) -> p k d", k=64)
    WH = abf(24576, 7168).rearrange("p (k c) -> p k c", k=8)
    WOH = abf(31744, 2048).rearrange("p (c d) -> p c d", c=2)
    HTG = [abf(33792 + i * 4096, 4096).rearrange("p (k t) -> p k t", k=8) for i in range(2)]
    MASK = abf(41984, 2048).rearrange("p (a t) -> p a t", a=4)
    PT = [abf(44032 + i * 512, 512) for i in range(4)]
    QTA = [abf(46080 + i * 512, 512) for i in range(2)]
    RQT, RQTD, RKT = abf(69632, 512), abf(47616, 512), abf(48128, 512)
    RKDEC = abf(48640, 512).rearrange("p (t d) -> p t d", t=4)
    RV = abf(49152, 512).rearrange("p (t d) -> p t d", t=4)
    AT = [abf(49664 + i * 128, 128) for i in range(4)]
    ROF, DOF = abf(50176, 512), abf(50688, 512)
    SQB = abf(66560, 512)
    QKN = [abf(67072 + i * 512, 512).rearrange("p (j c) -> p j c", j=4) for i in range(4)]
    STBF = abf(51712, 128)
    FB = 51840
    f32t = lambda k: af32(FB + k * 1024, 512)
    SG, RO32, CEN, SQ, RR, O1, O2, DO32 = (f32t(k) for k in range(8))
    RL = [f32t(8), f32t(9)]
    YSB = [af32(FB + 10240 + i * 2048, 1024) for i in range(2)]

    S.dma("pool", WH, L["wh_d"].rearrange("(k p) c -> p k c", p=128), w=["WH"])
    S.dma("pool", WOH, L["woh_d"].rearrange("(c p) d -> p c d", p=128), w=["WOH"])
    S.dma("pool", MASK, L["maskT_d"].rearrange("p (a t) -> p a t", a=4), w=["MASK"])
    S.op("dve", lambda e: e.memset(abf(67072, 2048), 0.0), w=[("QKN", t) for t in range(4)])
    for t in range(4):
        for j in range(4):
            srcap = L["qaug_d"][:, t * 64:(t + 1) * 64] if j < 2 else L["kaug_d"]
            S.dma("pool", QKN[t][:, j, 64:128], srcap, w=[("QKN", t)])
    S.op("dve", lambda e: e.memset(state_f[:, :], 0.0), w=["state_f"])
    S.op("dve", lambda e: e.memset(STBF, 0.0), w=["STBF"])

    def load_htg(g):
        s, rp = g // 4, g % 4
        S.dma("sp", HTG[g % 2], cg[s][rp * 1024:(rp + 1) * 1024, :].rearrange("(p k) t -> p k t", k=8),
              r=[("cg", s)], w=[("HTG", g % 2)])

    def lnexp(ap, keys, scale):
        S.op("act", lambda e: e.activation(out=ap, in_=ap, func=AF.Ln), r=keys, w=keys)
        S.op("act", lambda e: e.activation(out=ap, in_=ap, func=AF.Exp, scale=scale), r=keys, w=keys)

    if "nogroups" not in MIXOPT:
        load_htg(0)
    pt_ctr = [0]
    lvl = int(os.environ.get("GSTOP", "9"))
    for g in range(0 if "nogroups" in MIXOPT else int(os.environ.get("NGROUPS", "16"))):
        s, rp = g // 4, g % 4
        hg = HTG[g % 2]
        hk = ("HTG", g % 2)
        for t in range(4 if lvl >= 1 else 0):
            for kc in range(8):
                S.op("pe", lambda e, kc=kc, t=t, hg=hg: e.matmul(
                    out=ps[6][:, :], lhsT=hg[:, kc, t * 128:(t + 1) * 128], rhs=WH[:, kc, 256:768],
                    start=(kc == 0), stop=(kc == 7)), r=[hk, "WH"], w=[("ps", 6)])
            for kc in range(8):
                S.op("pe", lambda e, kc=kc, t=t, hg=hg: e.matmul(
                    out=ps[7][:, 0:128], lhsT=hg[:, kc, t * 128:(t + 1) * 128], rhs=WH[:, kc, 768:896],
                    start=(kc == 0), stop=(kc == 7)), r=[hk, "WH"], w=[("ps", 7)])
            S.op("act", lambda e, t=t: e.activation(
                out=RKDEC[:, t, :], in_=ps[6][:, 0:128], func=AF.Copy, scale=cst[:, C_KDEC:C_KDEC + 1]),
                r=[("ps", 6), "cst"], w=[("RKDEC", t)])
            S.op("act", lambda e, t=t: e.activation(out=RV[:, t, :], in_=ps[6][:, 128:256], func=AF.Copy),
                 r=[("ps", 6)], w=[("RV", t)])
            kt_g = 4 * g + t
            S.op("act", lambda e, kt_g=kt_g: e.activation(out=VC[:, kt_g, :], in_=ps[7][:, 0:128], func=AF.Copy),
                 r=[("ps", 7)], w=[("VC", kt_g)])
            if lvl < 2:
                continue
            S.op("act", lambda e: e.activation(out=SQ[:, 0:256], in_=ps[6][:, 256:512], func=AF.Square),
                 r=[("ps", 6)], w=["SQ"])
            c = rot(4)
            ck = [("mst", c + j) for j in range(4)]
            S.op("dve", lambda e, c=c: e.tensor_reduce(
                out=mst[:, c:c + 4], in_=SQ[:, 0:256].rearrange("p (g d) -> p g d", g=4), axis=AX.X, op=ALU.add),
                r=["SQ"], w=ck)
            S.op("dve", lambda e, c=c: e.tensor_scalar(
                out=mst[:, c:c + 4], in0=mst[:, c:c + 4], scalar1=1.0 / 64, scalar2=EPS,
                op0=ALU.mult, op1=ALU.add), r=ck, w=ck)
            lnexp(mst[:, c:c + 4], ck, -0.5)
            qb = QKN[t]
            for j in range(4):
                S.op("dve", lambda e, j=j, c=c, qb=qb: e.scalar_tensor_tensor(
                    out=qb[:, j, 0:64], in0=ps[6][:, 256 + j * 64:256 + (j + 1) * 64],
                    scalar=mst[:, c + j:c + j + 1], in1=cst[:, C_QKG + j * 64:C_QKG + (j + 1) * 64],
                    op0=ALU.mult, op1=ALU.mult), r=[("ps", 6), "cst"] + ck, w=[("QKN", t)])
            if lvl < 3:
                continue
            pv = psbf(5)
            for j in range(4):
                S.op("pe", lambda e, j=j, qb=qb, pv=pv: e.transpose(
                    out=pv[:, j * 128:(j + 1) * 128], in_=qb[:, j, :],
                    identity=ident[:, :]), r=[("QKN", t), "ident"], w=[("ps", 5)])
            for m in range(0 if "noqkevac" in MIXOPT else 2):
                S.op("dve", lambda e, m=m, t=t, pv=pv: e.tensor_copy(
                    out=QTA[m][:, t * 128:(t + 1) * 128], in_=pv[:, m * 128:(m + 1) * 128]),
                    r=[("ps", 5)], w=[("QTA", m)])
                S.op("dve", lambda e, m=m, kt_g=kt_g, pv=pv: e.tensor_copy(
                    out=KTA[m][:, kt_g * 128:(kt_g + 1) * 128], in_=pv[:, (2 + m) * 128:(3 + m) * 128]),
                    r=[("ps", 5)], w=[("KTA", m, kt_g)])
        if lvl < 4:
            continue
        for kc in range(8):
            S.op("pe", lambda e, kc=kc, hg=hg: e.matmul(out=ps[6][:, :], lhsT=WH[:, kc, 0:128], rhs=hg[:, kc, :],
                                                 start=(kc == 0), stop=(kc == 7)), r=[hk, "WH"], w=[("ps", 6)])
        if "rqt" not in SK:
            S.op("dve", lambda e: e.tensor_tensor(out=RQT, in0=ps[6][:, :], in1=L["ones512"][:, :], op=ALU.mult),
                 r=[("ps", 6), "ones512"], w=["RQT"])
        if "rqtd" not in SK:
            S.op("dve", lambda e: e.tensor_tensor(out=RQTD, in0=ps[6][:, :], in1=qdec[:, :], op=ALU.mult),
                 r=[("ps", 6), "qdec"], w=["RQTD"])
        for kc in range(8):
            S.op("pe", lambda e, kc=kc, hg=hg: e.matmul(out=ps[7][:, :], lhsT=WH[:, kc, 128:256], rhs=hg[:, kc, :],
                                                 start=(kc == 0), stop=(kc == 7)),
                 r=[hk, "WH"], w=[("ps", 7)])
        if "g1" not in SK:
            S.op("act", lambda e: e.activation(out=SG, in_=ps[7][:, :], func=AF.Exp, scale=-1.0),
                 r=[("ps", 7)], w=["SG"])
        if "g2" not in SK:
            S.op("dve", lambda e: e.tensor_scalar(out=SG, in0=SG, scalar1=1.0, scalar2=None, op0=ALU.add),
                 r=["SG"], w=["SG"])
        if "g3" not in SK:
            lnexp(SG, ["SG"], -1.0)
        if "g4" not in SK:
            S.op("dve", lambda e: e.tensor_tensor(out=SG, in0=SG, in1=ps[7][:, :], op=ALU.mult),
                 r=["SG", ("ps", 7)], w=["SG"])
        for kc in range(8):
            S.op("pe", lambda e, kc=kc, hg=hg: e.matmul(out=ps[6][:, :], lhsT=WH[:, kc, 256:384], rhs=hg[:, kc, :],
                                                 start=(kc == 0), stop=(kc == 7)), r=[hk, "WH"], w=[("ps", 6)])
        if "rkt" not in SK:
            S.op("dve", lambda e: e.tensor_copy(out=RKT, in_=ps[6][:, :]), r=[("ps", 6)], w=["RKT"])
        if g + 1 < 16 and "lh1" not in SK:
            load_htg(g + 1)
        for n in range(0 if "noret" in MIXOPT else 4):
            S.op("pe", lambda e, n=n: e.matmul(out=ps[0][:, n * 128:(n + 1) * 128],
                                               lhsT=RKT[:, n * 128:(n + 1) * 128],
                                               rhs=RQT[:, n * 128:(n + 1) * 128], start=True, stop=True),
                 r=["RKT", "RQT"], w=[("ps", 0)])
            S.op("pe", lambda e, n=n: e.matmul(out=ps[1][:, n * 128:(n + 1) * 128], lhsT=RKDEC[:, n, :],
                                               rhs=RV[:, n, :], start=True, stop=True),
                 r=[("RKDEC", n), ("RV", n)], w=[("ps", 1)])
        for n in range(0 if "noret" in MIXOPT else 4):
            S.op("dve", lambda e, n=n: e.tensor_tensor(out=AT[n], in0=ps[0][:, n * 128:(n + 1) * 128],
                                                       in1=decT[:, :], op=ALU.mult),
                 r=[("ps", 0), "decT"], w=[("AT", n)])
        for n in range(0 if "noret" in MIXOPT else 4):
            S.op("pe", lambda e, n=n: e.matmul(out=ps[6][:, n * 128:(n + 1) * 128], lhsT=RV[:, n, :], rhs=AT[n],
                                               start=True, stop=False),
                 r=[("RV", n), ("AT", n)], w=[("ps", 6)])
            S.op("pe", lambda e, n=n: e.matmul(out=ps[6][:, n * 128:(n + 1) * 128], lhsT=STBF,
                                               rhs=RQTD[:, n * 128:(n + 1) * 128], start=False, stop=True),
                 r=["STBF", "RQTD"], w=[("ps", 6)])
            S.op("dve", lambda e, n=n: e.scalar_tensor_tensor(
                out=state_f[:, :], in0=state_f[:, :], scalar=cst[:, C_G128:C_G128 + 1],
                in1=ps[1][:, n * 128:(n + 1) * 128], op0=ALU.mult, op1=ALU.add),
                r=["state_f", ("ps", 1), "cst"], w=["state_f"])
            S.op("dve", lambda e: e.tensor_copy(out=STBF, in_=state_f[:, :]),
                 r=["state_f"], w=["STBF"])
        if lvl < 5:
            continue
        S.op("dve", lambda e: e.tensor_copy(out=RO32, in_=ps[6][:, :]), r=[("ps", 6)], w=["RO32"])
        S.op("dve", lambda e: e.tensor_copy(out=SQB, in_=ps[6][:, :]), r=[("ps", 6)], w=["SQB"])
        S.op("pe", lambda e: e.matmul(out=ps[7][:, :], lhsT=ones_bf[:, :], rhs=SQB, start=True, stop=True),
             r=["ones_bf", "SQB"], w=[("ps", 7)])
        S.op("dve", lambda e: e.scalar_tensor_tensor(out=CEN, in0=ps[7][:, :], scalar=-1.0 / 128, in1=RO32,
                                                     op0=ALU.mult, op1=ALU.add),
             r=[("ps", 7), "RO32"], w=["CEN"])
        S.op("act", lambda e: e.activation(out=SQB, in_=CEN, func=AF.Square), r=["CEN"], w=["SQB"])
        S.op("pe", lambda e: e.matmul(out=ps[7][:, :], lhsT=ones_bf[:, :], rhs=SQB, start=True, stop=True),
             r=["ones_bf", "SQB"], w=[("ps", 7)])
        S.op("dve", lambda e: e.tensor_scalar(out=RR, in0=ps[7][:, :], scalar1=1.0 / 128, scalar2=EPS,
                                              op0=ALU.mult, op1=ALU.add),
             r=[("ps", 7)], w=["RR"])
        lnexp(RR, ["RR"], -0.5)
        S.op("dve", lambda e: e.tensor_tensor(out=CEN, in0=CEN, in1=RR, op=ALU.mult), r=["CEN", "RR"], w=["CEN"])
        S.op("dve", lambda e: e.scalar_tensor_tensor(out=ROF, in0=CEN, scalar=cst[:, C_RETN:C_RETN + 1], in1=SG,
                                                     op0=ALU.mult, op1=ALU.mult),
             r=["CEN", "SG", "cst"], w=["ROF"])
        if lvl < 6:
            continue
        nkt = 4 * g + 4
        for m in range(0 if "noattn" in MIXOPT else 2):
            ob, lb = 2 + 2 * m, 3 + 2 * m

            def emit_S(kt, m=m):
                sbk = kt % 2
                diag = kt >= 4 * g
                S.op("pe", lambda e: e.matmul(out=ps[sbk][:, :], lhsT=KTA[m][:, kt * 128:(kt + 1) * 128],
                                              rhs=QTA[m][:, :], start=True, stop=not diag),
                     r=[("KTA", m, kt), ("QTA", m)], w=[("ps", sbk)])
                if diag:
                    a = kt - 4 * g
                    S.op("pe", lambda e: e.matmul(out=ps[sbk][:, :], lhsT=ident[:, :], rhs=MASK[:, a, :],
                                                  start=False, stop=True),
                         r=["ident", "MASK"], w=[("ps", sbk)])

            emit_S(0)
            for kt in range(nkt):
                if kt + 1 < nkt:
                    emit_S(kt + 1)
                pb = pt_ctr[0] % 4
                pt_ctr[0] += 1
                bidx = 4 * g - kt + 3
                S.op("act", lambda e, kt=kt, pb=pb, bidx=bidx: e.activation(
                    out=PT[pb], in_=ps[kt % 2][:, :], func=AF.Exp, bias=biasT[:, bidx:bidx + 1], scale=1.0),
                    r=[("ps", kt % 2), "biasT"], w=[("PT", pb)])
                S.op("pe", lambda e, kt=kt, pb=pb, ob=ob, nkt=nkt: e.matmul(
                    out=ps[ob][:, :], lhsT=VC[:, kt, :], rhs=PT[pb], start=(kt == 0), stop=(kt == nkt - 1)),
                    r=[("VC", kt), ("PT", pb)], w=[("ps", ob)])
                S.op("pe", lambda e, kt=kt, pb=pb, lb=lb, nkt=nkt: e.matmul(
                    out=ps[lb][:, :], lhsT=ones_bf[:, :], rhs=PT[pb], start=(kt == 0), stop=(kt == nkt - 1)),
                    r=["ones_bf", ("PT", pb)], w=[("ps", lb)])
            S.op("act", lambda e, m=m, lb=lb: e.activation(out=RL[m], in_=ps[lb][:, :], func=AF.Ln),
                 r=[("ps", lb)], w=[("RL", m)])
            S.op("act", lambda e, m=m: e.activation(out=RL[m], in_=RL[m], func=AF.Exp, scale=-1.0),
                 r=[("RL", m)], w=[("RL", m)])
            Om = O1 if m == 0 else O2
            S.op("dve", lambda e, m=m, ob=ob, Om=Om: e.tensor_tensor(out=Om, in0=ps[ob][:, :], in1=RL[m],
                                                                    op=ALU.mult),
                 r=[("ps", ob), ("RL", m)], w=[("Om", m)])
        S.op("dve", lambda e: e.scalar_tensor_tensor(out=DO32, in0=O2, scalar=cst[:, C_NLAM:C_NLAM + 1], in1=O1,
                                                     op0=ALU.mult, op1=ALU.add),
             r=[("Om", 0), ("Om", 1), "cst"], w=["DO32"])
        S.op("act", lambda e: e.activation(out=SQB, in_=DO32, func=AF.Square), r=["DO32"], w=["SQB"])
        S.op("pe", lambda e: e.matmul(out=ps[7][:, :], lhsT=ones_bf[:, :], rhs=SQB, start=True, stop=True),
             r=["ones_bf", "SQB"], w=[("ps", 7)])
        S.op("dve", lambda e: e.tensor_scalar(out=RR, in0=ps[7][:, :], scalar1=1.0 / 128, scalar2=EPS,
                                              op0=ALU.mult, op1=ALU.add),
             r=[("ps", 7)], w=["RR"])
        lnexp(RR, ["RR"], -0.5)
        S.op("dve", lambda e: e.scalar_tensor_tensor(out=DOF, in0=DO32, scalar=cst[:, C_DFN08:C_DFN08 + 1], in1=RR,
                                                     op0=ALU.mult, op1=ALU.mult),
             r=["DO32", "RR", "cst"], w=["DOF"])
        if lvl < 7:
            continue
        for t in range(4):
            for half in range(2):
                pb = 6 + half
                pk = [("ps", 6)] if half == 0 else [("ps", 7)]
                S.op("pe", lambda e, t=t, half=half, pb=pb: e.matmul(
                    out=ps[pb][:, :], lhsT=ROF[:, t * 128:(t + 1) * 128], rhs=WOH[:, 0, half * 512:(half + 1) * 512],
                    start=True, stop=False), r=["ROF", "WOH"], w=pk)
                S.op("pe", lambda e, t=t, half=half, pb=pb: e.matmul(
                    out=ps[pb][:, :], lhsT=DOF[:, t * 128:(t + 1) * 128], rhs=WOH[:, 1, half * 512:(half + 1) * 512],
                    start=False, stop=True), r=["DOF", "WOH"], w=pk)
                yb = YSB[t % 2]
                if half == 0:
                    S.op("dve", lambda e, yb=yb, pb=pb: e.tensor_copy(out=yb[:, 0:512], in_=ps[pb][:, :]),
                         r=pk, w=[("YSB", t % 2, 0)])
                else:
                    S.op("dve", lambda e, yb=yb, pb=pb: e.tensor_copy(out=yb[:, 512:1024], in_=ps[pb][:, :]),
                         r=pk, w=[("YSB", t % 2, 1)])
            S.dma("sp", Yd[s][rp * 512 + t * 128:rp * 512 + (t + 1) * 128, :], YSB[t % 2],
                  r=[("YSB", t % 2, 0), ("YSB", t % 2, 1)], w=[("Yd", s)])
        if rp == 3:
            S.cc("ReduceScatter", ALU.add, [Yd[s]], [Zd[s]], r=[("Yd", s)], w=[("Zd", s)])

    S.barrier()
    S.dma("sp", xs.rearrange("p t d -> p (t d)"), xsp_d, r=["xspill"], w=[("xs", t) for t in range(NT)])
    ZST = [af32(HT + i * 8192, 4096).rearrange("p (t d) -> p t d", t=4) for i in range(2)]
    for s in range(4):
        z = ZST[s % 2]
        S.dma("sp", z, Zd[s].rearrange("(t p) d -> p t d", p=128), r=[("Zd", s)], w=[("ZST", s % 2)])
        for i in range(4):
            lt = 4 * s + i
            S.op("dve", lambda e, z=z, i=i, lt=lt: e.tensor_tensor(out=xs[:, lt, :], in0=xs[:, lt, :],
                                                                  in1=z[:, i, :], op=ALU.add),
                 r=[("xs", lt), ("ZST", s % 2)], w=[("xs", lt)])
    S.barrier()


def mem_attn(nc, S, L):
    abf, af32, ps, psbf = L["abf"], L["af32"], L["ps"], L["psbf"]
    xs, hT, cst = L["xs"], L["hT"], L["cst"]
    ident, ones_bf = L["ident"], L["ones_bf"]
    hn, junk, mkT, mv, sil = L["hn"], L["junk"], L["mkT"], L["mv"], L["sil"]
    load_gain, norm_transpose, evac_copy = L["load_gain"], L["norm_transpose"], L["evac_copy"]
    C_MQN = L["C_MQN"]
    HT, GT, W2P = L["HT"], L["GT"], L["W2P"]
    mst, rot = L["mst"], L["rot_cols"]

    S.barrier()
    load_gain(2)
    norm_transpose(xs, lambda t: ("xs", t), NT, hT, "hT", [6, 7])
    S.op("dve", lambda e: e.tensor_scalar(out=cst[:, C_MQN:C_MQN + 256], in0=cst[:, C_MQN:C_MQN + 256],
                                          scalar1=1.0 / 16, scalar2=None, op0=ALU.mult), r=["cst"], w=["cst"])
    qT = abf(GT, 16384).rearrange("p (k t) -> p k t", k=8)
    wqh = [abf(W2P + i * 4096, 4096).rearrange("p (k c) -> p k c", k=8) for i in range(2)]
    for half in range(2):
        S.dma("pool", wqh[half], L["wq_d"][:, half * 512:(half + 1) * 512].rearrange("(k p) c -> p k c", p=128),
              w=[("wqh", half)])
    for lt in range(NT):
        c = rot(4)
        ck = [("mst", c + j) for j in range(4)]
        for half in range(2):
            pb = 4 + half
            for kc in range(8):
                S.op("pe", lambda e, kc=kc, lt=lt, half=half, pb=pb: e.matmul(
                    out=ps[pb][:, :], lhsT=hT[:, kc, lt * 128:(lt + 1) * 128], rhs=wqh[half][:, kc, :],
                    start=(kc == 0), stop=(kc == 7)), r=[("hT", lt), ("wqh", half)], w=[("ps", pb)])
            for hh in range(2):
                hd = half * 2 + hh
                S.op("act", lambda e, hh=hh, pb=pb, hd=hd, c=c: e.activation(
                    out=junk[:, 0:256], in_=ps[pb][:, hh * 256:(hh + 1) * 256], func=AF.Square,
                    accum_out=mst[:, c + hd:c + hd + 1]), r=[("ps", pb)], w=["junk", ("mst", c + hd)])
        S.op("dve", lambda e, c=c: e.tensor_scalar(out=mst[:, c:c + 4], in0=mst[:, c:c + 4], scalar1=1.0 / 256,
                                                   scalar2=EPS, op0=ALU.mult, op1=ALU.add), r=ck, w=ck)
        S.op("act", lambda e, c=c: e.activation(out=mst[:, c:c + 4], in_=mst[:, c:c + 4], func=AF.Ln), r=ck, w=ck)
        S.op("act", lambda e, c=c: e.activation(out=mst[:, c:c + 4], in_=mst[:, c:c + 4], func=AF.Exp, scale=-0.5),
             r=ck, w=ck)
        hb = lt % 2
        for hd in range(4):
            pb = 4 + hd // 2
            hh = hd % 2
            S.op("dve", lambda e, hd=hd, hh=hh, pb=pb, hb=hb, c=c: e.scalar_tensor_tensor(
                out=hn[:, hb, hd * 256:(hd + 1) * 256], in0=ps[pb][:, hh * 256:(hh + 1) * 256],
                scalar=mst[:, c + hd:c + hd + 1], in1=cst[:, C_MQN:C_MQN + 256], op0=ALU.mult, op1=ALU.mult),
                r=[("ps", pb), "cst"] + ck, w=[("hn", hb)])
        pbt = 6 + lt % 2
        pv = psbf(pbt)
        for kc in range(8):
            S.op("pe", lambda e, kc=kc, hb=hb, pv=pv: e.transpose(
                out=pv[:, kc * 128:(kc + 1) * 128], in_=hn[:, hb, kc * 128:(kc + 1) * 128], identity=ident[:, :]),
                r=[("hn", hb), "ident"], w=[("ps", pbt)])
        evac_copy(qT[:, :, lt * 128:(lt + 1) * 128], pv[:, 0:1024].rearrange("p (k t) -> p k t", k=8),
                  r=[("ps", pbt)], w=[("qT", lt)])
    wo_t = abf(HT, 8192).rearrange("p (k c) -> p k c", k=8)
    OTG = abf(HT + 8192, 4096).rearrange("p (k t) -> p k t", k=8)
    PTm = [abf(HT + 12288 + i * 512, 512) for i in range(4)]
    S.dma("pool", wo_t, L["wo_d"].rearrange("(k p) c -> p k c", p=128),
          r=[], w=["wo_t"] + [("hT", t) for t in range(NT)])
    RLm = sil[:, 0, :]
    for tg in range(4):
        qk = [("qT", lt) for lt in range(4 * tg, 4 * tg + 4)]
        for hd in range(4):
            for kt in range(2):
                pi = (hd % 2) * 2 + kt
                for c2 in range(2):
                    S.op("pe", lambda e, kt=kt, c2=c2, hd=hd, tg=tg: e.matmul(
                        out=ps[kt][:, :], lhsT=mkT[:, hd * 2 + c2, kt * 128:(kt + 1) * 128],
                        rhs=qT[:, hd * 2 + c2, tg * 512:(tg + 1) * 512], start=(c2 == 0), stop=(c2 == 1)),
                        r=["mkT"] + qk, w=[("ps", kt)])
                S.op("act", lambda e, kt=kt, pi=pi: e.activation(out=PTm[pi], in_=ps[kt][:, :], func=AF.Exp),
                     r=[("ps", kt)], w=[("PTm", pi)])
            for dc in range(2):
                for kt in range(2):
                    pi = (hd % 2) * 2 + kt
                    S.op("pe", lambda e, kt=kt, dc=dc, hd=hd, pi=pi: e.matmul(
                        out=ps[2 + dc][:, :], lhsT=mv[:, kt, hd * 256 + dc * 128:hd * 256 + (dc + 1) * 128],
                        rhs=PTm[pi], start=(kt == 0), stop=(kt == 1)), r=["mv", ("PTm", pi)], w=[("ps", 2 + dc)])
            for kt in range(2):
                pi = (hd % 2) * 2 + kt
                S.op("pe", lambda e, kt=kt, pi=pi: e.matmul(out=ps[4][:, :], lhsT=ones_bf[:, :], rhs=PTm[pi],
                                                            start=(kt == 0), stop=(kt == 1)),
                     r=["ones_bf", ("PTm", pi)], w=[("ps", 4)])
            S.op("act", lambda e: e.activation(out=RLm, in_=ps[4][:, :], func=AF.Ln), r=[("ps", 4)], w=["RLm"])
            S.op("act", lambda e: e.activation(out=RLm, in_=RLm, func=AF.Exp, scale=-1.0), r=["RLm"], w=["RLm"])
            for dc in range(2):
                S.op("dve", lambda e, dc=dc, hd=hd: e.tensor_tensor(out=OTG[:, hd * 2 + dc, :], in0=ps[2 + dc][:, :],
                                                                    in1=RLm, op=ALU.mult),
                     r=[("ps", 2 + dc), "RLm"], w=[("OTG", hd * 2 + dc)])
        for l4 in range(4):
            lt = tg * 4 + l4
            for half in range(2):
                pb = 6 + half
                for c in range(8):
                    S.op("pe", lambda e, c=c, l4=l4, half=half, pb=pb: e.matmul(
                        out=ps[pb][:, :], lhsT=OTG[:, c, l4 * 128:(l4 + 1) * 128],
                        rhs=wo_t[:, c, half * 512:(half + 1) * 512], start=(c == 0), stop=(c == 7)),
                        r=[("OTG", c), "wo_t"], w=[("ps", pb)])
                S.op("dve", lambda e, lt=lt, half=half, pb=pb: e.tensor_tensor(
                    out=xs[:, lt, half * 512:(half + 1) * 512], in0=ps[pb][:, :],
                    in1=xs[:, lt, half * 512:(half + 1) * 512], op=ALU.add),
                    r=[("ps", pb), ("xs", lt)], w=[("xs", lt)])
    S.barrier()


def _head_consts(h):
    gamma = 1.0 - 2.0 ** (-5.0 - h)
    lg = math.log1p(-2.0 ** (-5.0 - h))
    idx = np.arange(128, dtype=np.float64)
    rel = idx[None, :] - idx[:, None]
    decT = np.where(rel >= 0, np.exp(lg * np.maximum(rel, 0.0)), 0.0) * (128 ** -0.5)
    qdec = np.tile(np.exp(lg * (idx + 1.0))[None, :], (128, 4))
    kdec = (np.exp(lg * (127.0 - idx)) * (128 ** -0.5))[:, None]
    g128 = np.full((128, 1), math.exp(lg * 128.0))
    m = 2.0 ** (-8.0 * (h + 1) / 4.0)
    p = np.arange(128)
    kaug = np.zeros((128, 64))
    kaug[:, 0] = m * p
    kaug[:, 1] = 1.0
    kaug[:, 2] = 1.0
    qaug = np.zeros((128, 256))
    for t in range(4):
        ii = t * 128 + p
        qaug[:, t * 64 + 0] = 1.0
        qaug[:, t * 64 + 1] = -m * (ii % 256)
        qaug[:, t * 64 + 2] = -m * 256.0 * (ii // 256)
    biasT = np.tile((-m * 128.0 * (np.arange(64) - 3.0))[None, :], (128, 1))
    a = np.arange(4)[None, :, None]
    j = np.arange(128)[:, None, None]
    i = np.arange(512)[None, None, :]
    maskT = np.where(i < a * 128 + j, -30000.0, 0.0).reshape(128, 2048)
    f = lambda v: np.ascontiguousarray(v, dtype=np.float32)
    return dict(decT=f(decT), qdec=f(qdec), kdec=f(kdec), g128=f(g128), kaug_tm=f(kaug), qaug_tm=f(qaug),
                biasT=f(biasT), maskT=f(maskT))


def make_in_maps(inp):
    g = lambda k: np.asarray(inp[k], dtype=np.float32)
    x = g("x")
    mem = g("mem")
    gains = np.ascontiguousarray(np.stack([g("norm_ffn1")[0], g("norm_mix")[0], g("norm_mem_q")[0],
                                           g("norm_mem_kv")[0], g("norm_ffn2")[0], g("norm_out")[0]]))
    w_in = g("w_in")[0]
    w_out = g("w_out")[0]
    ident = np.eye(128, dtype=np.float32)
    maps = []
    for c in range(8):
        b, r = c // 4, c % 4
        h = r
        xl = np.ascontiguousarray(x[b].reshape(4, 4, 512, D)[:, r].reshape(2048, D))
        cols = np.concatenate([
            np.arange(0 * 512 + h * 128, 0 * 512 + (h + 1) * 128),
            np.arange(3 * 512 + h * 128, 3 * 512 + (h + 1) * 128),
            np.arange(1 * 512 + h * 128, 1 * 512 + (h + 1) * 128),
            np.arange(2 * 512 + h * 128, 2 * 512 + (h + 1) * 128),
            np.arange(2048 + h * 128, 2048 + (h + 1) * 128),
            np.arange(2560 + h * 128, 2560 + (h + 1) * 128),
            np.arange(3072 + h * 128, 3072 + (h + 1) * 128),
        ])
        m = dict(
            x=xl, mem=np.ascontiguousarray(mem[b]), gains=gains,
            w1a=g("ffn1_w1")[0], w3a=g("ffn1_w3")[0], w2a=g("ffn1_w2")[0],
            w1b=g("ffn2_w1")[0], w3b=g("ffn2_w3")[0], w2b=g("ffn2_w2")[0],
            w_head=np.ascontiguousarray(w_in[:, cols]),
            wout_head=np.ascontiguousarray(np.concatenate(
                [w_out[h * 128:(h + 1) * 128], w_out[512 + h * 128:512 + (h + 1) * 128]], axis=0)),
            wq=g("mem_w_q")[0], wkv=g("mem_w_kv")[0], wo=g("mem_w_o")[0],
            ret_norm=np.ascontiguousarray(g("ret_norm")[0][:, None]),
            diff_norm=np.ascontiguousarray(g("diff_norm")[0][:, None]),
            qk_gain=np.ascontiguousarray(np.concatenate(
                [g("diff_q_norm")[0], g("diff_q_norm")[0], g("diff_k_norm")[0], g("diff_k_norm")[0]])[None, :]),
            lam_vecs=np.ascontiguousarray(np.concatenate(
                [g("lambda_q1")[0], g("lambda_k1")[0], g("lambda_q2")[0], g("lambda_k2")[0]])[None, :]),
            mem_q_norm=np.ascontiguousarray(g("mem_q_norm")[0][None, :]),
            mem_k_norm=np.ascontiguousarray(g("mem_k_norm")[0][None, :]),
            ident=ident,
        )
        m.update(_head_consts(h))
        maps.append(m)
    return maps


def assemble(results, key="y"):
    out = np.zeros((2, 4, 4, 512, D), dtype=np.float32)
    for c in range(8):
        b, r = c // 4, c % 4
        out[b, :, r] = np.asarray(results[c][key]).reshape(4, 512, D)
    return out.reshape(2, 8192, D)


_NC_CACHE = {}


def kernel(**inputs):
    if "nc" not in _NC_CACHE:
        _NC_CACHE["nc"] = build_program()
    nc = _NC_CACHE["nc"]
    in_maps = make_in_maps(inputs)
    res = run_bass_kernel_spmd(nc, in_maps, core_ids=list(range(8)))
    return assemble(res.results)
```

```python
import math
import contextlib
import numpy as np
import ml_dtypes
import concourse.bass as bass
import concourse.mybir as mybir
from concourse.bass_utils import run_bass_kernel_spmd

F32 = mybir.dt.float32
BF16 = mybir.dt.bfloat16
AF = mybir.ActivationFunctionType
ALU = mybir.AluOpType
AX = mybir.AxisListType

D = 1024
DFF = 2816
NT = 16
EPS = 1e-6
LAM_INIT = 0.8 - 0.6 * math.exp(-0.3 * 0)
NDMA = 8
import os
MIXOPT = set(os.environ.get("MIXOPT", "").split(","))
SK = set(os.environ.get("SKIP", "").split(","))
GROUPS = [[0, 1, 2, 3], [4, 5, 6, 7]]


class Sched:
    COMPUTE = ("pe", "act", "dve", "pool")

    def __init__(self, nc):
        self.nc = nc
        self.ops = []
        self.lastw = {}
        self.readers = {}
        self.pending = {}
        self.bar_from = 0

    def barrier(self):
        tails = set()
        lastc = {}
        for i in range(len(self.ops)):
            o = self.ops[i]
            if o["kind"] == "c":
                lastc[o["q"]] = i
            elif i >= self.bar_from:
                tails.add(i)
        tails.update(lastc.values())
        self.bar_from = len(self.ops)
        for q in ("sp", "pe", "act", "dve", "pool"):
            self.pending[q] = set(tails) | self.pending.get(q, set())

    def _add(self, q, kind, fn, r, w):
        idx = len(self.ops)
        deps = set()
        if q in self.pending:
            deps |= self.pending.pop(q)
        for k in r:
            lw = self.lastw.get(k)
            if lw is not None:
                deps.add(lw)
        for k in w:
            lw = self.lastw.get(k)
            if lw is not None:
                deps.add(lw)
            rd = self.readers.get(k)
            if rd:
                for v in rd.values():
                    if isinstance(v, list):
                        deps.update(v)
                    else:
                        deps.add(v)
        self.ops.append(dict(q=q, kind=kind, fn=fn, deps=deps))
        for k in r:
            rd = self.readers.setdefault(k, {})
            if kind == "c":
                rd[q] = idx
            else:
                rd.setdefault("x", []).append(idx)
        for k in w:
            self.lastw[k] = idx
            self.readers[k] = {}
        return idx

    def op(self, q, fn, r=(), w=()):
        return self._add(q, "c", fn, r, w)

    def dma(self, q, out, in_, r=(), w=()):
        return self._add(q, "dma", lambda e: e.dma_start(out=out, in_=in_), r, w)

    def cc(self, kind, alu, ins, outs, r=(), w=()):
        if "nocc" in MIXOPT:
            return None
        return self._add("pool", "cc", lambda e: e.collective_compute(
            kind, alu, replica_groups=GROUPS, ins=ins, outs=outs), r, w)

    def emit(self, final_wait_ops):
        nc = self.nc
        ops = self.ops
        needed = set()
        for o in ops:
            for d in o["deps"]:
                p = ops[d]
                if p["kind"] == "c" and p["q"] == "pe" and o["q"] == "pe" and o["kind"] == "c":
                    continue
                needed.add(d)
        needed.update(final_wait_ops)
        cnt = {q: 0 for q in self.COMPUTE}
        dcnt = {}
        drr = {}
        ncc = 0
        for i, o in enumerate(ops):
            q = o["q"]
            if o["kind"] == "c":
                if i in needed:
                    cnt[q] += 1
                    o["sem"] = ("c", q)
                    o["val"] = cnt[q]
                else:
                    o["sem"] = None
            elif o["kind"] == "dma":
                k = drr.get(q, 0) % NDMA
                drr[q] = drr.get(q, 0) + 1
                prev = dcnt.get((q, k), 0)
                o["prev"] = prev
                dcnt[(q, k)] = prev + 16
                o["sem"] = ("d", q, k)
                o["val"] = prev + 16
            else:
                o["sem"] = ("cc", ncc)
                o["val"] = 1
                ncc += 1
        semnames = set(o["sem"] for o in ops if o["sem"] is not None)
        stack = contextlib.ExitStack()
        semh = {}
        for sn in sorted(semnames, key=str):
            semh[sn] = stack.enter_context(nc.semaphore("s_" + "_".join(str(x) for x in sn)))
        queues = ("sp", "pe", "act", "dve", "pool")
        per_q = {q: [i for i, o in enumerate(ops) if o["q"] == q] for q in queues}

        def run(eng, q):
            seen = {}
            for i in per_q[q]:
                o = ops[i]
                waits = {}
                for d in o["deps"]:
                    p = ops[d]
                    if p["kind"] == "c" and p["q"] == "pe" and q == "pe" and o["kind"] == "c":
                        continue
                    s = p["sem"]
                    waits[s] = max(waits.get(s, 0), p["val"])
                if o["kind"] == "dma" and o["prev"] > 0:
                    s = o["sem"]
                    waits[s] = max(waits.get(s, 0), o["prev"])
                for s, v in waits.items():
                    if seen.get(s, 0) < v:
                        eng.wait_ge(semh[s], v)
                        seen[s] = v
                ins = o["fn"](eng)
                if o["sem"] is not None:
                    ins.then_inc(semh[o["sem"]], 16 if o["kind"] == "dma" else 1)
            if q == "sp":
                for i in final_wait_ops:
                    o = ops[i]
                    if seen.get(o["sem"], 0) < o["val"]:
                        eng.wait_ge(semh[o["sem"]], o["val"])
                        seen[o["sem"]] = o["val"]

        with stack, nc.Block() as block:
            @block.sync
            def _(e):
                run(e, "sp")

            @block.tensor
            def _(e):
                run(e, "pe")

            @block.scalar
            def _(e):
                run(e, "act")

            @block.vector
            def _(e):
                run(e, "dve")

            @block.gpsimd
            def _(e):
                run(e, "pool")


def build_program(stages=("kv", "ffn1", "mix", "mem", "ffn2"), dbg=()):
    nc = bass.Bass("TRN2", target_bir_lowering=False)
    S = Sched(nc)

    def din(name, shape, dt=F32):
        return nc.dram_tensor(name, shape, dt, kind="ExternalInput").ap()

    x_d = din("x", [2048, D])
    mem_d = din("mem", [256, D])
    gains_d = din("gains", [6, D])
    wf = {}
    for f in ("a", "b"):
        wf[f] = (din("w1" + f, [22, 128, D]), din("w3" + f, [22, 128, D]), din("w2" + f, [DFF, D]))
    wh_d = din("w_head", [D, 896])
    woh_d = din("wout_head", [256, D])
    wq_d = din("wq", [D, D])
    wkv_d = din("wkv", [D, 2 * D])
    wo_d = din("wo", [D, D])
    retn_d = din("ret_norm", [128, 1])
    dfn_d = din("diff_norm", [128, 1])
    qkg_d = din("qk_gain", [1, 256])
    lamv_d = din("lam_vecs", [1, 256])
    mqn_d = din("mem_q_norm", [1, 256])
    mkn_d = din("mem_k_norm", [1, 256])
    decT_d = din("decT", [128, 128])
    qdec_d = din("qdec", [128, 512])
    kdec_d = din("kdec", [128, 1])
    g128_d = din("g128", [128, 1])
    kaug_d = din("kaug_tm", [128, 64])
    qaug_d = din("qaug_tm", [128, 256])
    biasT_d = din("biasT", [128, 64])
    maskT_d = din("maskT", [128, 2048])
    ident_d = din("ident", [128, 128])
    y_d = nc.dram_tensor("y", [2048, D], F32, kind="ExternalOutput").ap()
    dbg_d = {}
    for name, shape in dbg:
        dbg_d[name] = nc.dram_tensor(name, list(shape), F32, kind="ExternalOutput").ap()

    cin = [nc.dram_tensor(f"cin{s}", [1024, 512], BF16, kind="Internal").ap() for s in range(4)]
    cg = [nc.dram_tensor(f"cg{s}", [4096, 512], BF16, kind="Internal").ap() for s in range(4)]
    Yd = [nc.dram_tensor(f"Yd{s}", [2048, D], F32, kind="Internal").ap() for s in range(4)]
    Zd = [nc.dram_tensor(f"Zd{s}", [512, D], F32, kind="Internal").ap() for s in range(4)]
    xsp_d = nc.dram_tensor("xspill", [128, NT * D], F32, kind="Internal").ap()

    st = contextlib.ExitStack()

    def sb(name, shape, dt):
        return st.enter_context(nc.sbuf_tensor(name, shape, dt))

    ARENA = 73728
    arena = sb("arena", [128, ARENA], BF16)

    def abf(off, n):
        return arena[:, off:off + n]

    def af32(off, n):
        return arena[:, off:off + 2 * n].bitcast(F32)

    xs = af32(0, NT * D).rearrange("p (t d) -> p t d", t=NT)
    HT, GT, W13, W2P = 32768, 49152, 57344, 65536
    hT = abf(HT, 16384).rearrange("p (k t) -> p k t", k=8)
    gT = abf(GT, 8192).rearrange("p (c t) -> p c t", c=4)
    w13b = [abf(W13 + i * 2048, 2048).rearrange("p (k c) -> p k c", k=8) for i in range(4)]
    w13c = [abf(W13 + i * 2048, 2048).rearrange("p (c f) -> p c f", c=2) for i in range(4)]
    w2b = [abf(W2P + i * 4096, 4096).rearrange("p (c d) -> p c d", c=4) for i in range(2)]

    gbc = sb("gbc", [128, D], F32)
    hn = sb("hn", [128, 2, D], BF16)
    ident = sb("ident_sb", [128, 128], BF16)
    ones_bf = sb("ones_bf", [128, 128], BF16)
    ones_f = sb("ones_f", [128, 128], F32)
    ones512 = sb("ones512", [128, 512], F32)
    stats = sb("stats", [128, 512], F32)
    sil = sb("sil", [128, 2, 512], F32)
    junk = sb("junk", [128, D], BF16)
    mkT = sb("mkT", [128, 8, 256], BF16)
    mv = sb("mv", [128, 2, D], BF16)
    cst = sb("cst", [128, 1024], F32)
    decT = sb("decT_sb", [128, 128], F32)
    qdec = sb("qdec_sb", [128, 512], F32)
    biasT = sb("biasT_sb", [128, 64], F32)
    state_f = sb("state_f", [128, 128], F32)
    ps = [st.enter_context(nc.psum_tensor(f"ps{i}", [128, 512], F32)) for i in range(8)]

    def psbf(i):
        return ps[i][:, :].bitcast(BF16)

    C_RETN, C_DFN, C_KDEC, C_G128, C_NLAM, C_TMP = 0, 1, 2, 3, 4, 8
    C_QKG = 16
    C_LAMV = 272
    C_MQN = 528
    C_MKN = 784
    mkn_t = sb("mkn_t", [128, 256], F32)

    mst = sb("mst", [128, 256], F32)
    rot_ctr = [0]

    def rot_cols(n):
        o = rot_ctr[0]
        if o + n > 256:
            o = 0
        rot_ctr[0] = o + n
        return o

    stat_ctr = [0]

    def stat_cols(n):
        o = stat_ctr[0]
        stat_ctr[0] += n
        assert stat_ctr[0] <= 512
        return o

    S.dma("pool", ident[:, :], ident_d, w=["ident"])
    S.op("dve", lambda e: e.memset(ones_bf[:, :], 1.0), w=["ones_bf"])
    S.op("dve", lambda e: e.memset(ones_f[:, :], 1.0), w=["ones_f"])
    S.op("dve", lambda e: e.memset(ones512[:, :], 1.0), w=["ones512"])
    S.dma("sp", cst[:, C_RETN:C_RETN + 1], retn_d, w=["cst"])
    S.dma("sp", cst[:, C_DFN:C_DFN + 1], dfn_d, w=["cst"])
    S.dma("sp", cst[:, C_KDEC:C_KDEC + 1], kdec_d, w=["cst"])
    S.dma("sp", cst[:, C_G128:C_G128 + 1], g128_d, w=["cst"])
    S.dma("sp", cst[:, C_QKG:C_QKG + 256], qkg_d.partition_broadcast(128), w=["cst"])
    S.dma("sp", cst[:, C_LAMV:C_LAMV + 256], lamv_d.partition_broadcast(128), w=["cst"])
    S.dma("sp", cst[:, C_MQN:C_MQN + 256], mqn_d.partition_broadcast(128), w=["cst"])
    S.dma("sp", mkn_t[:, :], mkn_d.partition_broadcast(128), w=["mkn"])
    S.dma("sp", decT[:, :], decT_d, w=["decT"])
    S.dma("sp", qdec[:, :], qdec_d, w=["qdec"])
    S.dma("sp", biasT[:, :], biasT_d, w=["biasT"])
    for t in range(NT):
        S.dma("sp", xs[:, t, :], x_d[t * 128:(t + 1) * 128, :], w=[("xs", t)])

    def load_gain(row):
        S.dma("sp", gbc[:, :], gains_d[row:row + 1, :].partition_broadcast(128), w=["gbc"])

    cp_tog = [0]

    def evac_copy(out, in_, r, w):
        cp_tog[0] ^= 1
        if cp_tog[0]:
            S.op("act", lambda e: e.activation(out=out, in_=in_, func=AF.Copy), r=r, w=w)
        else:
            S.op("dve", lambda e: e.tensor_copy(out=out, in_=in_), r=r, w=w)

    def rms_rstd(src_tiles, ncols_total, inv_n, tag):
        n = len(src_tiles)
        c_ss = stat_cols(n)
        c_rs = stat_cols(n)
        for i, (ap, rk) in enumerate(src_tiles):
            S.op("act", lambda e, ap=ap, i=i: e.activation(
                out=junk[:, 0:ap.shape[-1]], in_=ap, func=AF.Square,
                accum_out=stats[:, c_ss + i:c_ss + i + 1]),
                r=rk, w=["junk", ("st", c_ss + i)])
        keys_ss = [("st", c_ss + i) for i in range(n)]
        keys_rs = [("st", c_rs + i) for i in range(n)]
        S.op("dve", lambda e: e.tensor_scalar(
            out=stats[:, c_rs:c_rs + n], in0=stats[:, c_ss:c_ss + n], scalar1=inv_n, scalar2=EPS,
            op0=ALU.mult, op1=ALU.add), r=keys_ss, w=keys_rs)
        S.op("act", lambda e: e.activation(out=stats[:, c_rs:c_rs + n], in_=stats[:, c_rs:c_rs + n],
                                           func=AF.Ln), r=keys_rs, w=keys_rs)
        S.op("act", lambda e: e.activation(out=stats[:, c_rs:c_rs + n], in_=stats[:, c_rs:c_rs + n],
                                           func=AF.Exp, scale=-0.5), r=keys_rs, w=keys_rs)
        return c_rs

    def norm_transpose(src, src_keys, ntiles, dstT, dst_key, pbank):
        c_rs = rms_rstd([(src[:, t, :], [src_keys(t)]) for t in range(ntiles)], ntiles, 1.0 / D, "n")
        for t in range(ntiles):
            hb = t % 2
            S.op("dve", lambda e, t=t, hb=hb: e.scalar_tensor_tensor(
                out=hn[:, hb, :], in0=src[:, t, :], scalar=stats[:, c_rs + t:c_rs + t + 1],
                in1=gbc[:, :], op0=ALU.mult, op1=ALU.mult),
                r=[src_keys(t), ("st", c_rs + t), "gbc"], w=[("hn", hb)])
            pb = pbank[t % len(pbank)]
            pv = psbf(pb)
            for kc in range(8):
                S.op("pe", lambda e, kc=kc, hb=hb, pv=pv: e.transpose(
                    out=pv[:, kc * 128:(kc + 1) * 128], in_=hn[:, hb, kc * 128:(kc + 1) * 128],
                    identity=ident[:, :]), r=[("hn", hb), "ident"], w=[("ps", pb)])
            evac_copy(dstT[:, :, t * 128:(t + 1) * 128],
                      pv[:, 0:1024].rearrange("p (k t) -> p k t", k=8),
                      r=[("ps", pb)], w=[(dst_key, t)])

    if "kv" in stages:
        memx = af32(GT, 2 * D).rearrange("p (t d) -> p t d", t=2)
        for t in range(2):
            S.dma("sp", memx[:, t, :], mem_d[t * 128:(t + 1) * 128, :], w=[("memx", t)])
        load_gain(3)
        memT = hT
        norm_transpose(memx, lambda t: ("memx", t), 2, memT, "hT", [6, 7])
        for cb in range(4):
            wt = abf(W13 + (cb % 2) * 4096, 4096).rearrange("p (k c) -> p k c", k=8)
            S.dma("pool", wt, wkv_d[:, cb * 512:(cb + 1) * 512].rearrange("(k p) c -> p k c", p=128),
                  w=[("w13", (cb % 2) * 2), ("w13", (cb % 2) * 2 + 1)])
            for mt in range(2):
                pb = 4 + mt
                for kc in range(8):
                    S.op("pe", lambda e, kc=kc, mt=mt, wt=wt, pb=pb: e.matmul(
                        out=ps[pb][:, :], lhsT=memT[:, kc, mt * 128:(mt + 1) * 128], rhs=wt[:, kc, :],
                        start=(kc == 0), stop=(kc == 7)),
                        r=[("hT", mt), ("w13", (cb % 2) * 2), ("w13", (cb % 2) * 2 + 1)], w=[("ps", pb)])
                if cb < 2:
                    c_ss = stat_cols(2)
                    c_rs = stat_cols(2)
                    for hh in range(2):
                        S.op("act", lambda e, hh=hh, pb=pb, c_ss=c_ss: e.activation(
                            out=junk[:, 0:256], in_=ps[pb][:, hh * 256:(hh + 1) * 256], func=AF.Square,
                            accum_out=stats[:, c_ss + hh:c_ss + hh + 1]),
                            r=[("ps", pb)], w=["junk", ("st", c_ss + hh)])
                    kss = [("st", c_ss), ("st", c_ss + 1)]
                    krs = [("st", c_rs), ("st", c_rs + 1)]
                    S.op("dve", lambda e, c_ss=c_ss, c_rs=c_rs: e.tensor_scalar(
                        out=stats[:, c_rs:c_rs + 2], in0=stats[:, c_ss:c_ss + 2], scalar1=1.0 / 256,
                        scalar2=EPS, op0=ALU.mult, op1=ALU.add), r=kss, w=krs)
                    S.op("act", lambda e, c_rs=c_rs: e.activation(
                        out=stats[:, c_rs:c_rs + 2], in_=stats[:, c_rs:c_rs + 2], func=AF.Ln), r=krs, w=krs)
                    S.op("act", lambda e, c_rs=c_rs: e.activation(
                        out=stats[:, c_rs:c_rs + 2], in_=stats[:, c_rs:c_rs + 2], func=AF.Exp, scale=-0.5),
                        r=krs, w=krs)
                    for hh in range(2):
                        S.op("dve", lambda e, hh=hh, pb=pb, c_rs=c_rs: e.scalar_tensor_tensor(
                            out=hn[:, 0, hh * 256:(hh + 1) * 256], in0=ps[pb][:, hh * 256:(hh + 1) * 256],
                            scalar=stats[:, c_rs + hh:c_rs + hh + 1], in1=mkn_t[:, :],
                            op0=ALU.mult, op1=ALU.mult),
                            r=[("ps", pb), ("st", c_rs + hh), "mkn"], w=[("hn", 0)])
                    pv = psbf(7 if pb != 7 else 6)
                    pbt = 7 if pb != 7 else 6
                    for j in range(4):
                        S.op("pe", lambda e, j=j, pv=pv: e.transpose(
                            out=pv[:, j * 128:(j + 1) * 128], in_=hn[:, 0, j * 128:(j + 1) * 128],
                            identity=ident[:, :]), r=[("hn", 0), "ident"], w=[("ps", pbt)])
                    evac_copy(mkT[:, cb * 4:(cb + 1) * 4, mt * 128:(mt + 1) * 128],
                              pv[:, 0:512].rearrange("p (k t) -> p k t", k=4),
                              r=[("ps", pbt)], w=["mkT"])
                else:
                    evac_copy(mv[:, mt, (cb - 2) * 512:(cb - 1) * 512], ps[pb][:, :],
                              r=[("ps", pb)], w=["mv"])

        S.barrier()

    PIECES = [(0, 4), (4, 4), (8, 4), (12, 4), (16, 3), (19, 3)]

    def ffn(which, gain_row):
        w1_d, w3_d, w2_d = wf[which]
        load_gain(gain_row)
        norm_transpose(xs, lambda t: ("xs", t), NT, hT, "hT", [6, 7])
        hT_keys = [("hT", t) for t in range(NT)]
        sub_ctr = [0]
        for pi, (c0, ncs) in enumerate(PIECES):
            w2t = w2b[pi % 2]
            S.dma("pool", w2t[:, 0:ncs, :],
                  w2_d[c0 * 128:(c0 + ncs) * 128, :].rearrange("(c p) d -> p c d", p=128),
                  w=[("w2p", pi % 2)])
            subs = [(c0 + i, min(2, ncs - i)) for i in range(0, ncs, 2)]
            for (sc, sn) in subs:
                sbi = sub_ctr[0] % 2
                sub_ctr[0] += 1
                w1t, w3t = w13c[sbi * 2], w13c[sbi * 2 + 1]
                S.dma("pool", w1t[:, 0:sn, :], w1_d[sc:sc + sn].rearrange("c p f -> p c f"),
                      w=[("w13", sbi * 2)])
                S.dma("pool", w3t[:, 0:sn, :], w3_d[sc:sc + sn].rearrange("c p f -> p c f"),
                      w=[("w13", sbi * 2 + 1)])
                for cc in range(sn):
                    cl = sc + cc - c0
                    for tg in range(4):
                        bu, bv = (0, 1) if (tg % 2 == 0) else (2, 3)
                        for kc in range(8):
                            S.op("pe", lambda e, kc=kc, cc=cc, tg=tg, w1t=w1t, bu=bu: e.matmul(
                                out=ps[bu][:, :], lhsT=w1t[:, cc, kc * 128:(kc + 1) * 128],
                                rhs=hT[:, kc, tg * 512:(tg + 1) * 512], start=(kc == 0), stop=(kc == 7)),
                                r=[("w13", sbi * 2)] + hT_keys[tg * 4:tg * 4 + 4], w=[("ps", bu)])
                        for kc in range(8):
                            S.op("pe", lambda e, kc=kc, cc=cc, tg=tg, w3t=w3t, bv=bv: e.matmul(
                                out=ps[bv][:, :], lhsT=w3t[:, cc, kc * 128:(kc + 1) * 128],
                                rhs=hT[:, kc, tg * 512:(tg + 1) * 512], start=(kc == 0), stop=(kc == 7)),
                                r=[("w13", sbi * 2 + 1)] + hT_keys[tg * 4:tg * 4 + 4], w=[("ps", bv)])
                        sb_i = tg % 2
                        S.op("act", lambda e, bu=bu, sb_i=sb_i: e.activation(
                            out=sil[:, sb_i, :], in_=ps[bu][:, :], func=AF.Silu),
                            r=[("ps", bu)], w=[("sil", sb_i)])
                        S.op("dve", lambda e, bv=bv, sb_i=sb_i, cl=cl, tg=tg: e.tensor_tensor(
                            out=gT[:, cl, tg * 512:(tg + 1) * 512], in0=sil[:, sb_i, :], in1=ps[bv][:, :],
                            op=ALU.mult), r=[("sil", sb_i), ("ps", bv)], w=[("gT", cl, tg)])
            for t in range(NT):
                for half in range(2):
                    pb = 4 + (t * 2 + half) % 2
                    for cl in range(ncs):
                        S.op("pe", lambda e, cl=cl, t=t, half=half, pb=pb, w2t=w2t, ncs=ncs: e.matmul(
                            out=ps[pb][:, :], lhsT=gT[:, cl, t * 128:(t + 1) * 128],
                            rhs=w2t[:, cl, half * 512:(half + 1) * 512], start=(cl == 0), stop=(cl == ncs - 1)),
                            r=[("gT", cl, t // 4), ("w2p", pi % 2)], w=[("ps", pb)])
                    S.op("dve", lambda e, t=t, half=half, pb=pb: e.scalar_tensor_tensor(
                        out=xs[:, t, half * 512:(half + 1) * 512], in0=ps[pb][:, :], scalar=0.5,
                        in1=xs[:, t, half * 512:(half + 1) * 512], op0=ALU.mult, op1=ALU.add),
                        r=[("ps", pb), ("xs", t)], w=[("xs", t)])

    if "ffn1" in stages:
        ffn("a", 0)

    if "mix" in stages:
        mixer(nc, S, locals())

    if "mem" in stages:
        mem_attn(nc, S, locals())

    if "ffn2" in stages:
        ffn("b", 4)

    S.barrier()
    load_gain(5)
    c_rs = rms_rstd([(xs[:, t, :], [("xs", t)]) for t in range(NT)], NT, 1.0 / D, "o")
    outs = []
    yb = sil[:, :, :].rearrange("p a b -> p (a b)")
    ystage = [yb, af32(GT, D)]
    for t in range(NT):
        ysb = ystage[t % 2]
        S.op("dve", lambda e, t=t, ysb=ysb: e.scalar_tensor_tensor(
            out=ysb, in0=xs[:, t, :], scalar=stats[:, c_rs + t:c_rs + t + 1], in1=gbc[:, :],
            op0=ALU.mult, op1=ALU.mult), r=[("xs", t), ("st", c_rs + t), "gbc"], w=[("ystage", t % 2)])
        outs.append(S.dma("sp", y_d[t * 128:(t + 1) * 128, :], ysb, r=[("ystage", t % 2)]))
    S.emit(outs)
    st.close()
    return nc


def mixer(nc, S, L):
    abf, af32, ps, psbf = L["abf"], L["af32"], L["ps"], L["psbf"]
    xs, hT, cst, stats = L["xs"], L["hT"], L["cst"], L["stats"]
    ident, ones_bf, ones_f = L["ident"], L["ones_bf"], L["ones_f"]
    decT, qdec, biasT, state_f = L["decT"], L["qdec"], L["biasT"], L["state_f"]
    cin, cg, Yd, Zd, xsp_d = L["cin"], L["cg"], L["Yd"], L["Zd"], L["xsp_d"]
    load_gain, norm_transpose, evac_copy = L["load_gain"], L["norm_transpose"], L["evac_copy"]
    C_RETN, C_DFN, C_KDEC, C_G128, C_NLAM, C_TMP = (L[k] for k in
                                                   ("C_RETN", "C_DFN", "C_KDEC", "C_G128", "C_NLAM", "C_TMP"))
    C_QKG, C_LAMV = L["C_QKG"], L["C_LAMV"]
    HT = L["HT"]
    mst = L["mst"]
    rot = L["rot_cols"]
    C_DFN08 = 5

    load_gain(1)
    norm_transpose(xs, lambda t: ("xs", t), NT, hT, "hT", [6, 7])
    S.dma("sp", xsp_d, xs.rearrange("p t d -> p (t d)"), r=[("xs", t) for t in range(NT)], w=["xspill"])
    for s in range(4):
        S.dma("sp", cin[s].rearrange("(p k) t -> p k t", k=8), hT[:, :, s * 512:(s + 1) * 512],
              r=[("hT", t) for t in range(4 * s, 4 * s + 4)], w=[("cin", s)])
        S.cc("AllGather", ALU.bypass, [cin[s]], [cg[s]], r=[("cin", s)], w=[("cg", s)])
    S.op("dve", lambda e: e.tensor_scalar(out=cst[:, C_QKG:C_QKG + 128], in0=cst[:, C_QKG:C_QKG + 128],
                                          scalar1=0.125, scalar2=None, op0=ALU.mult), r=["cst"], w=["cst"])
    S.op("dve", lambda e: e.tensor_scalar(out=cst[:, C_DFN08:C_DFN08 + 1], in0=cst[:, C_DFN:C_DFN + 1],
                                          scalar1=1.0 - LAM_INIT, scalar2=None, op0=ALU.mult), r=["cst"], w=["cst"])
    for j in range(2):
        S.op("dve", lambda e, j=j: e.tensor_tensor(
            out=L["junk"][:, 0:64], in0=cst[:, C_LAMV + 128 * j:C_LAMV + 128 * j + 64],
            in1=cst[:, C_LAMV + 128 * j + 64:C_LAMV + 128 * j + 128], op=ALU.mult), r=["cst"], w=["junk"])
        S.op("dve", lambda e, j=j: e.tensor_reduce(
            out=cst[:, C_TMP + j:C_TMP + j + 1], in_=L["junk"][:, 0:64], axis=AX.X, op=ALU.add),
            r=["junk"], w=["cst"])
    S.op("act", lambda e: e.activation(out=cst[:, C_TMP:C_TMP + 2], in_=cst[:, C_TMP:C_TMP + 2], func=AF.Exp),
         r=["cst"], w=["cst"])
    S.op("dve", lambda e: e.tensor_tensor(out=cst[:, C_NLAM:C_NLAM + 1], in0=cst[:, C_TMP + 1:C_TMP + 2],
                                          in1=cst[:, C_TMP:C_TMP + 1], op=ALU.subtract), r=["cst"], w=["cst"])
    S.op("dve", lambda e: e.tensor_scalar(out=cst[:, C_NLAM:C_NLAM + 1], in0=cst[:, C_NLAM:C_NLAM + 1],
                                          scalar1=-LAM_INIT, scalar2=None, op0=ALU.add), r=["cst"], w=["cst"])

    S.barrier()
    KTA = [abf(0, 8192), abf(8192, 8192)]
    VC = abf(16384, 8192).rearrange("p (k d) -> p k d", k=64)
    WH = abf(24576, 7168).rearrange("p (k c) -> p k c", k=8)
    WOH = abf(31744, 2048).rearrange("p (c d) -> p c d", c=2)
    HTG = [abf(33792 + i * 4096, 4096).rearrange("p (k t) -> p k t", k=8) for i in range(2)]
    MASK = abf(41984, 2048).rearrange("p (a t) -> p a t", a=4)
    PT = [abf(44032 + i * 512, 512) for i in range(4)]
    QTA = [abf(46080 + i * 512, 512) for i in range(2)]
    RQT, RQTD, RKT = abf(69632, 512), abf(47616, 512), abf(48128, 512)
    RKDEC = abf(48640, 512).rearrange("p (t d) -> p t d", t=4)
    RV = abf(49152, 512).rearrange("p (t d) -> p t d", t=4)
    AT = [abf(49664 + i * 128, 128) for i in range(4)]
    ROF, DOF = abf(50176, 512), abf(50688, 512)
    SQB = abf(66560, 512)
    QKN = [abf(67072 + i * 512, 512).rearrange("p (j c) -> p j c", j=4) for i in range(4)]
    STBF = abf(51712, 128)
    FB = 51840
    f32t = lambda k: af32(FB + k * 1024, 512)
    SG, RO32, CEN, SQ, RR, O1, O2, DO32 = (f32t(k) for k in range(8))
    RL = [f32t(8), f32t(9)]
    YSB = [af32(FB + 10240 + i * 2048, 1024) for i in range(2)]

    S.dma("pool", WH, L["wh_d"].rearrange("(k p) c -> p k c", p=128), w=["WH"])
    S.dma("pool", WOH, L["woh_d"].rearrange("(c p) d -> p c d", p=128), w=["WOH"])
    S.dma("pool", MASK, L["maskT_d"].rearrange("p (a t) -> p a t", a=4), w=["MASK"])
    S.op("dve", lambda e: e.memset(abf(67072, 2048), 0.0), w=[("QKN", t) for t in range(4)])
    for t in range(4):
        for j in range(4):
            srcap = L["qaug_d"][:, t * 64:(t + 1) * 64] if j < 2 else L["kaug_d"]
            S.dma("pool", QKN[t][:, j, 64:128], srcap, w=[("QKN", t)])
    S.op("dve", lambda e: e.memset(state_f[:, :], 0.0), w=["state_f"])
    S.op("dve", lambda e: e.memset(STBF, 0.0), w=["STBF"])

    def load_htg(g):
        s, rp = g // 4, g % 4
        S.dma("sp", HTG[g % 2], cg[s][rp * 1024:(rp + 1) * 1024, :].rearrange("(p k) t -> p k t", k=8),
              r=[("cg", s)], w=[("HTG", g % 2)])

    def lnexp(ap, keys, scale):
        S.op("act", lambda e: e.activation(out=ap, in_=ap, func=AF.Ln), r=keys, w=keys)
        S.op("act", lambda e: e.activation(out=ap, in_=ap, func=AF.Exp, scale=scale), r=keys, w=keys)

    if "nogroups" not in MIXOPT:
        load_htg(0)
    pt_ctr = [0]
    lvl = int(os.environ.get("GSTOP", "9"))
    for g in range(0 if "nogroups" in MIXOPT else int(os.environ.get("NGROUPS", "16"))):
        s, rp = g // 4, g % 4
        hg = HTG[g % 2]
        hk = ("HTG", g % 2)
        for t in range(4 if lvl >= 1 else 0):
            for kc in range(8):
                S.op("pe", lambda e, kc=kc, t=t, hg=hg: e.matmul(
                    out=ps[6][:, :], lhsT=hg[:, kc, t * 128:(t + 1) * 128], rhs=WH[:, kc, 256:768],
                    start=(kc == 0), stop=(kc == 7)), r=[hk, "WH"], w=[("ps", 6)])
            for kc in range(8):
                S.op("pe", lambda e, kc=kc, t=t, hg=hg: e.matmul(
                    out=ps[7][:, 0:128], lhsT=hg[:, kc, t * 128:(t + 1) * 128], rhs=WH[:, kc, 768:896],
                    start=(kc == 0), stop=(kc == 7)), r=[hk, "WH"], w=[("ps", 7)])
            S.op("act", lambda e, t=t: e.activation(
                out=RKDEC[:, t, :], in_=ps[6][:, 0:128], func=AF.Copy, scale=cst[:, C_KDEC:C_KDEC + 1]),
                r=[("ps", 6), "cst"], w=[("RKDEC", t)])
            S.op("act", lambda e, t=t: e.activation(out=RV[:, t, :], in_=ps[6][:, 128:256], func=AF.Copy),
                 r=[("ps", 6)], w=[("RV", t)])
            kt_g = 4 * g + t
            S.op("act", lambda e, kt_g=kt_g: e.activation(out=VC[:, kt_g, :], in_=ps[7][:, 0:128], func=AF.Copy),
                 r=[("ps", 7)], w=[("VC", kt_g)])
            if lvl < 2:
                continue
            S.op("act", lambda e: e.activation(out=SQ[:, 0:256], in_=ps[6][:, 256:512], func=AF.Square),
                 r=[("ps", 6)], w=["SQ"])
            c = rot(4)
            ck = [("mst", c + j) for j in range(4)]
            S.op("dve", lambda e, c=c: e.tensor_reduce(
                out=mst[:, c:c + 4], in_=SQ[:, 0:256].rearrange("p (g d) -> p g d", g=4), axis=AX.X, op=ALU.add),
                r=["SQ"], w=ck)
            S.op("dve", lambda e, c=c: e.tensor_scalar(
                out=mst[:, c:c + 4], in0=mst[:, c:c + 4], scalar1=1.0 / 64, scalar2=EPS,
                op0=ALU.mult, op1=ALU.add), r=ck, w=ck)
            lnexp(mst[:, c:c + 4], ck, -0.5)
            qb = QKN[t]
            for j in range(4):
                S.op("dve", lambda e, j=j, c=c, qb=qb: e.scalar_tensor_tensor(
                    out=qb[:, j, 0:64], in0=ps[6][:, 256 + j * 64:256 + (j + 1) * 64],
                    scalar=mst[:, c + j:c + j + 1], in1=cst[:, C_QKG + j * 64:C_QKG + (j + 1) * 64],
                    op0=ALU.mult, op1=ALU.mult), r=[("ps", 6), "cst"] + ck, w=[("QKN", t)])
            if lvl < 3:
                continue
            pv = psbf(5)
            for j in range(4):
                S.op("pe", lambda e, j=j, qb=qb, pv=pv: e.transpose(
                    out=pv[:, j * 128:(j + 1) * 128], in_=qb[:, j, :],
                    identity=ident[:, :]), r=[("QKN", t), "ident"], w=[("ps", 5)])
            for m in range(0 if "noqkevac" in MIXOPT else 2):
                S.op("dve", lambda e, m=m, t=t, pv=pv: e.tensor_copy(
                    out=QTA[m][:, t * 128:(t + 1) * 128], in_=pv[:, m * 128:(m + 1) * 128]),
                    r=[("ps", 5)], w=[("QTA", m)])
                S.op("dve", lambda e, m=m, kt_g=kt_g, pv=pv: e.tensor_copy(
                    out=KTA[m][:, kt_g * 128:(kt_g + 1) * 128], in_=pv[:, (2 + m) * 128:(3 + m) * 128]),
                    r=[("ps", 5)], w=[("KTA", m, kt_g)])
        if lvl < 4:
            continue
        for kc in range(8):
            S.op("pe", lambda e, kc=kc, hg=hg: e.matmul(out=ps[6][:, :], lhsT=WH[:, kc, 0:128], rhs=hg[:, kc, :],
                                                 start=(kc == 0), stop=(kc == 7)), r=[hk, "WH"], w=[("ps", 6)])
        if "rqt" not in SK:
            S.op("dve", lambda e: e.tensor_tensor(out=RQT, in0=ps[6][:, :], in1=L["ones512"][:, :], op=ALU.mult),
                 r=[("ps", 6), "ones512"], w=["RQT"])
        if "rqtd" not in SK:
            S.op("dve", lambda e: e.tensor_tensor(out=RQTD, in0=ps[6][:, :], in1=qdec[:, :], op=ALU.mult),
                 r=[("ps", 6), "qdec"], w=["RQTD"])
        for kc in range(8):
            S.op("pe", lambda e, kc=kc, hg=hg: e.matmul(out=ps[7][:, :], lhsT=WH[:, kc, 128:256], rhs=hg[:, kc, :],
                                                 start=(kc == 0), stop=(kc == 7)),
                 r=[hk, "WH"], w=[("ps", 7)])
        if "g1" not in SK:
            S.op("act", lambda e: e.activation(out=SG, in_=ps[7][:, :], func=AF.Exp, scale=-1.0),
                 r=[("ps", 7)], w=["SG"])
        if "g2" not in SK:
            S.op("dve", lambda e: e.tensor_scalar(out=SG, in0=SG, scalar1=1.0, scalar2=None, op0=ALU.add),
                 r=["SG"], w=["SG"])
        if "g3" not in SK:
            lnexp(SG, ["SG"], -1.0)
        if "g4" not in SK:
            S.op("dve", lambda e: e.tensor_tensor(out=SG, in0=SG, in1=ps[7][:, :], op=ALU.mult),
                 r=["SG", ("ps", 7)], w=["SG"])
        for kc in range(8):
            S.op("pe", lambda e, kc=kc, hg=hg: e.matmul(out=ps[6][:, :], lhsT=WH[:, kc, 256:384], rhs=hg[:, kc, :],
                                                 start=(kc == 0), stop=(kc == 7)), r=[hk, "WH"], w=[("ps", 6)])
        if "rkt" not in SK:
            S.op("dve", lambda e: e.tensor_copy(out=RKT, in_=ps[6][:, :]), r=[("ps", 6)], w=["RKT"])
        if g + 1 < 16 and "lh1" not in SK:
            load_htg(g + 1)
        for n in range(0 if "noret" in MIXOPT else 4):
            S.op("pe", lambda e, n=n: e.matmul(out=ps[0][:, n * 128:(n + 1) * 128],
                                               lhsT=RKT[:, n * 128:(n + 1) * 128],
                                               rhs=RQT[:, n * 128:(n + 1) * 128], start=True, stop=True),
                 r=["RKT", "RQT"], w=[("ps", 0)])
            S.op("pe", lambda e, n=n: e.matmul(out=ps[1][:, n * 128:(n + 1) * 128], lhsT=RKDEC[:, n, :],
                                               rhs=RV[:, n, :], start=True, stop=True),
                 r=[("RKDEC", n), ("RV", n)], w=[("ps", 1)])
        for n in range(0 if "noret" in MIXOPT else 4):
            S.op("dve", lambda e, n=n: e.tensor_tensor(out=AT[n], in0=ps[0][:, n * 128:(n + 1) * 128],
                                                       in1=decT[:, :], op=ALU.mult),
                 r=[("ps", 0), "decT"], w=[("AT", n)])
        for n in range(0 if "noret" in MIXOPT else 4):
            S.op("pe", lambda e, n=n: e.matmul(out=ps[6][:, n * 128:(n + 1) * 128], lhsT=RV[:, n, :], rhs=AT[n],
                                               start=True, stop=False),
                 r=[("RV", n), ("AT", n)], w=[("ps", 6)])
            S.op("pe", lambda e, n=n: e.matmul(out=ps[6][:, n * 128:(n + 1) * 128], lhsT=STBF,
                                               rhs=RQTD[:, n * 128:(n + 1) * 128], start=False, stop=True),
                 r=["STBF", "RQTD"], w=[("ps", 6)])
            S.op("dve", lambda e, n=n: e.scalar_tensor_tensor(
                out=state_f[:, :], in0=state_f[:, :], scalar=cst[:, C_G128:C_G128 + 1],
                in1=ps[1][:, n * 128:(n + 1) * 128], op0=ALU.mult, op1=ALU.add),
                r=["state_f", ("ps", 1), "cst"], w=["state_f"])
            S.op("dve", lambda e: e.tensor_copy(out=STBF, in_=state_f[:, :]),
                 r=["state_f"], w=["STBF"])
        if lvl < 5:
            continue
        S.op("dve", lambda e: e.tensor_copy(out=RO32, in_=ps[6][:, :]), r=[("ps", 6)], w=["RO32"])
        S.op("dve", lambda e: e.tensor_copy(out=SQB, in_=ps[6][:, :]), r=[("ps", 6)], w=["SQB"])
        S.op("pe", lambda e: e.matmul(out=ps[7][:, :], lhsT=ones_bf[:, :], rhs=SQB, start=True, stop=True),
             r=["ones_bf", "SQB"], w=[("ps", 7)])
        S.op("dve", lambda e: e.scalar_tensor_tensor(out=CEN, in0=ps[7][:, :], scalar=-1.0 / 128, in1=RO32,
                                                     op0=ALU.mult, op1=ALU.add),
             r=[("ps", 7), "RO32"], w=["CEN"])
        S.op("act", lambda e: e.activation(out=SQB, in_=CEN, func=AF.Square), r=["CEN"], w=["SQB"])
        S.op("pe", lambda e: e.matmul(out=ps[7][:, :], lhsT=ones_bf[:, :], rhs=SQB, start=True, stop=True),
             r=["ones_bf", "SQB"], w=[("ps", 7)])
        S.op("dve", lambda e: e.tensor_scalar(out=RR, in0=ps[7][:, :], scalar1=1.0 / 128, scalar2=EPS,
                                              op0=ALU.mult, op1=ALU.add),
             r=[("ps", 7)], w=["RR"])
        lnexp(RR, ["RR"], -0.5)
        S.op("dve", lambda e: e.tensor_tensor(out=CEN, in0=CEN, in1=RR, op=ALU.mult), r=["CEN", "RR"], w=["CEN"])
        S.op("dve", lambda e: e.scalar_tensor_tensor(out=ROF, in0=CEN, scalar=cst[:, C_RETN:C_RETN + 1], in1=SG,
                                                     op0=ALU.mult, op1=ALU.mult),
             r=["CEN", "SG", "cst"], w=["ROF"])
        if lvl < 6:
            continue
        nkt = 4 * g + 4
        for m in range(0 if "noattn" in MIXOPT else 2):
            ob, lb = 2 + 2 * m, 3 + 2 * m

            def emit_S(kt, m=m):
                sbk = kt % 2
                diag = kt >= 4 * g
                S.op("pe", lambda e: e.matmul(out=ps[sbk][:, :], lhsT=KTA[m][:, kt * 128:(kt + 1) * 128],
                                              rhs=QTA[m][:, :], start=True, stop=not diag),
                     r=[("KTA", m, kt), ("QTA", m)], w=[("ps", sbk)])
                if diag:
                    a = kt - 4 * g
                    S.op("pe", lambda e: e.matmul(out=ps[sbk][:, :], lhsT=ident[:, :], rhs=MASK[:, a, :],
                                                  start=False, stop=True),
                         r=["ident", "MASK"], w=[("ps", sbk)])

            emit_S(0)
            for kt in range(nkt):
                if kt + 1 < nkt:
                    emit_S(kt + 1)
                pb = pt_ctr[0] % 4
                pt_ctr[0] += 1
                bidx = 4 * g - kt + 3
                S.op("act", lambda e, kt=kt, pb=pb, bidx=bidx: e.activation(
                    out=PT[pb], in_=ps[kt % 2][:, :], func=AF.Exp, bias=biasT[:, bidx:bidx + 1], scale=1.0),
                    r=[("ps", kt % 2), "biasT"], w=[("PT", pb)])
                S.op("pe", lambda e, kt=kt, pb=pb, ob=ob, nkt=nkt: e.matmul(
                    out=ps[ob][:, :], lhsT=VC[:, kt, :], rhs=PT[pb], start=(kt == 0), stop=(kt == nkt - 1)),
                    r=[("VC", kt), ("PT", pb)], w=[("ps", ob)])
                S.op("pe", lambda e, kt=kt, pb=pb, lb=lb, nkt=nkt: e.matmul(
                    out=ps[lb][:, :], lhsT=ones_bf[:, :], rhs=PT[pb], start=(kt == 0), stop=(kt == nkt - 1)),
                    r=["ones_bf", ("PT", pb)], w=[("ps", lb)])
            S.op("act", lambda e, m=m, lb=lb: e.activation(out=RL[m], in_=ps[lb][:, :], func=AF.Ln),
                 r=[("ps", lb)], w=[("RL", m)])
            S.op("act", lambda e, m=m: e.activation(out=RL[m], in_=RL[m], func=AF.Exp, scale=-1.0),
                 r=[("RL", m)], w=[("RL", m)])
            Om = O1 if m == 0 else O2
            S.op("dve", lambda e, m=m, ob=ob, Om=Om: e.tensor_tensor(out=Om, in0=ps[ob][:, :], in1=RL[m],
                                                                    op=ALU.mult),
                 r=[("ps", ob), ("RL", m)], w=[("Om", m)])
        S.op("dve", lambda e: e.scalar_tensor_tensor(out=DO32, in0=O2, scalar=cst[:, C_NLAM:C_NLAM + 1], in1=O1,
                                                     op0=ALU.mult, op1=ALU.add),
             r=[("Om", 0), ("Om", 1), "cst"], w=["DO32"])
        S.op("act", lambda e: e.activation(out=SQB, in_=DO32, func=AF.Square), r=["DO32"], w=["SQB"])
        S.op("pe", lambda e: e.matmul(out=ps[7][:, :], lhsT=ones_bf[:, :], rhs=SQB, start=True, stop=True),
             r=["ones_bf", "SQB"], w=[("ps", 7)])
        S.op("dve", lambda e: e.tensor_scalar(out=RR, in0=ps[7][:, :], scalar1=1.0 / 128, scalar2=EPS,
                                              op0=ALU.mult, op1=ALU.add),
             r=[("ps", 7)], w=["RR"])
        lnexp(RR, ["RR"], -0.5)
        S.op("dve", lambda e: e.scalar_tensor_tensor(out=DOF, in0=DO32, scalar=cst[:, C_DFN08:C_DFN08 + 1], in1=RR,
                                                     op0=ALU.mult, op1=ALU.mult),
             r=["DO32", "RR", "cst"], w=["DOF"])
        if lvl < 7:
            continue
        for t in range(4):
            for half in range(2):
                pb = 6 + half
                pk = [("ps", 6)] if half == 0 else [("ps", 7)]
                S.op("pe", lambda e, t=t, half=half, pb=pb: e.matmul(
                    out=ps[pb][:, :], lhsT=ROF[:, t * 128:(t + 1) * 128], rhs=WOH[:, 0, half * 512:(half + 1) * 512],
                    start=True, stop=False), r=["ROF", "WOH"], w=pk)
                S.op("pe", lambda e, t=t, half=half, pb=pb: e.matmul(
                    out=ps[pb][:, :], lhsT=DOF[:, t * 128:(t + 1) * 128], rhs=WOH[:, 1, half * 512:(half + 1) * 512],
                    start=False, stop=True), r=["DOF", "WOH"], w=pk)
                yb = YSB[t % 2]
                if half == 0:
                    S.op("dve", lambda e, yb=yb, pb=pb: e.tensor_copy(out=yb[:, 0:512], in_=ps[pb][:, :]),
                         r=pk, w=[("YSB", t % 2, 0)])
                else:
                    S.op("dve", lambda e, yb=yb, pb=pb: e.tensor_copy(out=yb[:, 512:1024], in_=ps[pb][:, :]),
                         r=pk, w=[("YSB", t % 2, 1)])
            S.dma("sp", Yd[s][rp * 512 + t * 128:rp * 512 + (t + 1) * 128, :], YSB[t % 2],
                  r=[("YSB", t % 2, 0), ("YSB", t % 2, 1)], w=[("Yd", s)])
        if rp == 3:
            S.cc("ReduceScatter", ALU.add, [Yd[s]], [Zd[s]], r=[("Yd", s)], w=[("Zd", s)])

    S.barrier()
    S.dma("sp", xs.rearrange("p t d -> p (t d)"), xsp_d, r=["xspill"], w=[("xs", t) for t in range(NT)])
    ZST = [af32(HT + i * 8192, 4096).rearrange("p (t d) -> p t d", t=4) for i in range(2)]
    for s in range(4):
        z = ZST[s % 2]
        S.dma("sp", z, Zd[s].rearrange("(t p) d -> p t d", p=128), r=[("Zd", s)], w=[("ZST", s % 2)])
        for i in range(4):
            lt = 4 * s + i
            S.op("dve", lambda e, z=z, i=i, lt=lt: e.tensor_tensor(out=xs[:, lt, :], in0=xs[:, lt, :],
                                                                  in1=z[:, i, :], op=ALU.add),
                 r=[("xs", lt), ("ZST", s % 2)], w=[("xs", lt)])
    S.barrier()


def mem_attn(nc, S, L):
    abf, af32, ps, psbf = L["abf"], L["af32"], L["ps"], L["psbf"]
    xs, hT, cst = L["xs"], L["hT"], L["cst"]
    ident, ones_bf = L["ident"], L["ones_bf"]
    hn, junk, mkT, mv, sil = L["hn"], L["junk"], L["mkT"], L["mv"], L["sil"]
    load_gain, norm_transpose, evac_copy = L["load_gain"], L["norm_transpose"], L["evac_copy"]
    C_MQN = L["C_MQN"]
    HT, GT, W2P = L["HT"], L["GT"], L["W2P"]
    mst, rot = L["mst"], L["rot_cols"]

    S.barrier()
    load_gain(2)
    norm_transpose(xs, lambda t: ("xs", t), NT, hT, "hT", [6, 7])
    S.op("dve", lambda e: e.tensor_scalar(out=cst[:, C_MQN:C_MQN + 256], in0=cst[:, C_MQN:C_MQN + 256],
                                          scalar1=1.0 / 16, scalar2=None, op0=ALU.mult), r=["cst"], w=["cst"])
    qT = abf(GT, 16384).rearrange("p (k t) -> p k t", k=8)
    wqh = [abf(W2P + i * 4096, 4096).rearrange("p (k c) -> p k c", k=8) for i in range(2)]
    for half in range(2):
        S.dma("pool", wqh[half], L["wq_d"][:, half * 512:(half + 1) * 512].rearrange("(k p) c -> p k c", p=128),
              w=[("wqh", half)])
    for lt in range(NT):
        c = rot(4)
        ck = [("mst", c + j) for j in range(4)]
        for half in range(2):
            pb = 4 + half
            for kc in range(8):
                S.op("pe", lambda e, kc=kc, lt=lt, half=half, pb=pb: e.matmul(
                    out=ps[pb][:, :], lhsT=hT[:, kc, lt * 128:(lt + 1) * 128], rhs=wqh[half][:, kc, :],
                    start=(kc == 0), stop=(kc == 7)), r=[("hT", lt), ("wqh", half)], w=[("ps", pb)])
            for hh in range(2):
                hd = half * 2 + hh
                S.op("act", lambda e, hh=hh, pb=pb, hd=hd, c=c: e.activation(
                    out=junk[:, 0:256], in_=ps[pb][:, hh * 256:(hh + 1) * 256], func=AF.Square,
                    accum_out=mst[:, c + hd:c + hd + 1]), r=[("ps", pb)], w=["junk", ("mst", c + hd)])
        S.op("dve", lambda e, c=c: e.tensor_scalar(out=mst[:, c:c + 4], in0=mst[:, c:c + 4], scalar1=1.0 / 256,
                                                   scalar2=EPS, op0=ALU.mult, op1=ALU.add), r=ck, w=ck)
        S.op("act", lambda e, c=c: e.activation(out=mst[:, c:c + 4], in_=mst[:, c:c + 4], func=AF.Ln), r=ck, w=ck)
        S.op("act", lambda e, c=c: e.activation(out=mst[:, c:c + 4], in_=mst[:, c:c + 4], func=AF.Exp, scale=-0.5),
             r=ck, w=ck)
        hb = lt % 2
        for hd in range(4):
            pb = 4 + hd // 2
            hh = hd % 2
            S.op("dve", lambda e, hd=hd, hh=hh, pb=pb, hb=hb, c=c: e.scalar_tensor_tensor(
                out=hn[:, hb, hd * 256:(hd + 1) * 256], in0=ps[pb][:, hh * 256:(hh + 1) * 256],
                scalar=mst[:, c + hd:c + hd + 1], in1=cst[:, C_MQN:C_MQN + 256], op0=ALU.mult, op1=ALU.mult),
                r=[("ps", pb), "cst"] + ck, w=[("hn", hb)])
        pbt = 6 + lt % 2
        pv = psbf(pbt)
        for kc in range(8):
            S.op("pe", lambda e, kc=kc, hb=hb, pv=pv: e.transpose(
                out=pv[:, kc * 128:(kc + 1) * 128], in_=hn[:, hb, kc * 128:(kc + 1) * 128], identity=ident[:, :]),
                r=[("hn", hb), "ident"], w=[("ps", pbt)])
        evac_copy(qT[:, :, lt * 128:(lt + 1) * 128], pv[:, 0:1024].rearrange("p (k t) -> p k t", k=8),
                  r=[("ps", pbt)], w=[("qT", lt)])
    wo_t = abf(HT, 8192).rearrange("p (k c) -> p k c", k=8)
    OTG = abf(HT + 8192, 4096).rearrange("p (k t) -> p k t", k=8)
    PTm = [abf(HT + 12288 + i * 512, 512) for i in range(4)]
    S.dma("pool", wo_t, L["wo_d"].rearrange("(k p) c -> p k c", p=128),
          r=[], w=["wo_t"] + [("hT", t) for t in range(NT)])
    RLm = sil[:, 0, :]
    for tg in range(4):
        qk = [("qT", lt) for lt in range(4 * tg, 4 * tg + 4)]
        for hd in range(4):
            for kt in range(2):
                pi = (hd % 2) * 2 + kt
                for c2 in range(2):
                    S.op("pe", lambda e, kt=kt, c2=c2, hd=hd, tg=tg: e.matmul(
                        out=ps[kt][:, :], lhsT=mkT[:, hd * 2 + c2, kt * 128:(kt + 1) * 128],
                        rhs=qT[:, hd * 2 + c2, tg * 512:(tg + 1) * 512], start=(c2 == 0), stop=(c2 == 1)),
                        r=["mkT"] + qk, w=[("ps", kt)])
                S.op("act", lambda e, kt=kt, pi=pi: e.activation(out=PTm[pi], in_=ps[kt][:, :], func=AF.Exp),
                     r=[("ps", kt)], w=[("PTm", pi)])
            for dc in range(2):
                for kt in range(2):
                    pi = (hd % 2) * 2 + kt
                    S.op("pe", lambda e, kt=kt, dc=dc, hd=hd, pi=pi: e.matmul(
                        out=ps[2 + dc][:, :], lhsT=mv[:, kt, hd * 256 + dc * 128:hd * 256 + (dc + 1) * 128],
                        rhs=PTm[pi], start=(kt == 0), stop=(kt == 1)), r=["mv", ("PTm", pi)], w=[("ps", 2 + dc)])
            for kt in range(2):
                pi = (hd % 2) * 2 + kt
                S.op("pe", lambda e, kt=kt, pi=pi: e.matmul(out=ps[4][:, :], lhsT=ones_bf[:, :], rhs=PTm[pi],
                                                            start=(kt == 0), stop=(kt == 1)),
                     r=["ones_bf", ("PTm", pi)], w=[("ps", 4)])
            S.op("act", lambda e: e.activation(out=RLm, in_=ps[4][:, :], func=AF.Ln), r=[("ps", 4)], w=["RLm"])
            S.op("act", lambda e: e.activation(out=RLm, in_=RLm, func=AF.Exp, scale=-1.0), r=["RLm"], w=["RLm"])
            for dc in range(2):
                S.op("dve", lambda e, dc=dc, hd=hd: e.tensor_tensor(out=OTG[:, hd * 2 + dc, :], in0=ps[2 + dc][:, :],
                                                                    in1=RLm, op=ALU.mult),
                     r=[("ps", 2 + dc), "RLm"], w=[("OTG", hd * 2 + dc)])
        for l4 in range(4):
            lt = tg * 4 + l4
            for half in range(2):
                pb = 6 + half
                for c in range(8):
                    S.op("pe", lambda e, c=c, l4=l4, half=half, pb=pb: e.matmul(
                        out=ps[pb][:, :], lhsT=OTG[:, c, l4 * 128:(l4 + 1) * 128],
                        rhs=wo_t[:, c, half * 512:(half + 1) * 512], start=(c == 0), stop=(c == 7)),
                        r=[("OTG", c), "wo_t"], w=[("ps", pb)])
                S.op("dve", lambda e, lt=lt, half=half, pb=pb: e.tensor_tensor(
                    out=xs[:, lt, half * 512:(half + 1) * 512], in0=ps[pb][:, :],
                    in1=xs[:, lt, half * 512:(half + 1) * 512], op=ALU.add),
                    r=[("ps", pb), ("xs", lt)], w=[("xs", lt)])
    S.barrier()


def _head_consts(h):
    gamma = 1.0 - 2.0 ** (-5.0 - h)
    lg = math.log1p(-2.0 ** (-5.0 - h))
    idx = np.arange(128, dtype=np.float64)
    rel = idx[None, :] - idx[:, None]
    decT = np.where(rel >= 0, np.exp(lg * np.maximum(rel, 0.0)), 0.0) * (128 ** -0.5)
    qdec = np.tile(np.exp(lg * (idx + 1.0))[None, :], (128, 4))
    kdec = (np.exp(lg * (127.0 - idx)) * (128 ** -0.5))[:, None]
    g128 = np.full((128, 1), math.exp(lg * 128.0))
    m = 2.0 ** (-8.0 * (h + 1) / 4.0)
    p = np.arange(128)
    kaug = np.zeros((128, 64))
    kaug[:, 0] = m * p
    kaug[:, 1] = 1.0
    kaug[:, 2] = 1.0
    qaug = np.zeros((128, 256))
    for t in range(4):
        ii = t * 128 + p
        qaug[:, t * 64 + 0] = 1.0
        qaug[:, t * 64 + 1] = -m * (ii % 256)
        qaug[:, t * 64 + 2] = -m * 256.0 * (ii // 256)
    biasT = np.tile((-m * 128.0 * (np.arange(64) - 3.0))[None, :], (128, 1))
    a = np.arange(4)[None, :, None]
    j = np.arange(128)[:, None, None]
    i = np.arange(512)[None, None, :]
    maskT = np.where(i < a * 128 + j, -30000.0, 0.0).reshape(128, 2048)
    f = lambda v: np.ascontiguousarray(v, dtype=np.float32)
    return dict(decT=f(decT), qdec=f(qdec), kdec=f(kdec), g128=f(g128), kaug_tm=f(kaug), qaug_tm=f(qaug),
                biasT=f(biasT), maskT=f(maskT))


def make_in_maps(inp):
    g = lambda k: np.asarray(inp[k], dtype=np.float32)
    x = g("x")
    mem = g("mem")
    gains = np.ascontiguousarray(np.stack([g("norm_ffn1")[0], g("norm_mix")[0], g("norm_mem_q")[0],
                                           g("norm_mem_kv")[0], g("norm_ffn2")[0], g("norm_out")[0]]))
    w_in = g("w_in")[0]
    w_out = g("w_out")[0]
    ident = np.eye(128, dtype=np.float32)
    _cm = {}

    def cm(k):
        if k not in _cm:
            _cm[k] = np.ascontiguousarray(g(k)[0].reshape(8, 128, 22, 128).transpose(2, 1, 0, 3).reshape(22, 128, D))
        return _cm[k]
    maps = []
    for c in range(8):
        b, r = c // 4, c % 4
        h = r
        xl = np.ascontiguousarray(x[b].reshape(4, 4, 512, D)[:, r].reshape(2048, D))
        cols = np.concatenate([
            np.arange(0 * 512 + h * 128, 0 * 512 + (h + 1) * 128),
            np.arange(3 * 512 + h * 128, 3 * 512 + (h + 1) * 128),
            np.arange(1 * 512 + h * 128, 1 * 512 + (h + 1) * 128),
            np.arange(2 * 512 + h * 128, 2 * 512 + (h + 1) * 128),
            np.arange(2048 + h * 128, 2048 + (h + 1) * 128),
            np.arange(2560 + h * 128, 2560 + (h + 1) * 128),
            np.arange(3072 + h * 128, 3072 + (h + 1) * 128),
        ])
        m = dict(
            x=xl, mem=np.ascontiguousarray(mem[b]), gains=gains,
            w1a=cm("ffn1_w1"), w3a=cm("ffn1_w3"), w2a=g("ffn1_w2")[0],
            w1b=cm("ffn2_w1"), w3b=cm("ffn2_w3"), w2b=g("ffn2_w2")[0],
            w_head=np.ascontiguousarray(w_in[:, cols]),
            wout_head=np.ascontiguousarray(np.concatenate(
                [w_out[h * 128:(h + 1) * 128], w_out[512 + h * 128:512 + (h + 1) * 128]], axis=0)),
            wq=g("mem_w_q")[0], wkv=g("mem_w_kv")[0], wo=g("mem_w_o")[0],
            ret_norm=np.ascontiguousarray(g("ret_norm")[0][:, None]),
            diff_norm=np.ascontiguousarray(g("diff_norm")[0][:, None]),
            qk_gain=np.ascontiguousarray(np.concatenate(
                [g("diff_q_norm")[0], g("diff_q_norm")[0], g("diff_k_norm")[0], g("diff_k_norm")[0]])[None, :]),
            lam_vecs=np.ascontiguousarray(np.concatenate(
                [g("lambda_q1")[0], g("lambda_k1")[0], g("lambda_q2")[0], g("lambda_k2")[0]])[None, :]),
            mem_q_norm=np.ascontiguousarray(g("mem_q_norm")[0][None, :]),
            mem_k_norm=np.ascontiguousarray(g("mem_k_norm")[0][None, :]),
            ident=ident,
        )
        m.update(_head_consts(h))
        maps.append(m)
    return maps


def assemble(results, key="y"):
    out = np.zeros((2, 4, 4, 512, D), dtype=np.float32)
    for c in range(8):
        b, r = c // 4, c % 4
        out[b, :, r] = np.asarray(results[c][key]).reshape(4, 512, D)
    return out.reshape(2, 8192, D)


_NC_CACHE = {}


def kernel(**inputs):
    if "nc" not in _NC_CACHE:
        _NC_CACHE["nc"] = build_program()
    nc = _NC_CACHE["nc"]
    in_maps = make_in_maps(inputs)
    res = run_bass_kernel_spmd(nc, in_maps, core_ids=list(range(8)))
    return assemble(res.results)
```
